# Optimizing a Trainium2 kernel written in Bass

```python
import jax, jax.numpy as jnp
from jax import lax
import numpy as np

D_MODEL = 1024
BATCH = 4
SEQ = 8192
DEPTH = 2

N_MIXERS = 2
D_FF = 4 * D_MODEL
EPS = 1e-6

GLA_HEADS = 4
GLA_DK = D_MODEL // 2
GLA_DV = D_MODEL
GLA_HEAD_K = GLA_DK // GLA_HEADS
GLA_HEAD_V = GLA_DV // GLA_HEADS
GLA_GATE_RANK = 16
GLA_GATE_NORM = 16.0
GLA_CHUNK = 64
GLA_IN = 2 * GLA_DK + 2 * GLA_DV + GLA_GATE_RANK

FOX_HEADS = 16
FOX_HEAD_DIM = D_MODEL // FOX_HEADS
FOX_BLOCK = 128
FOX_IN = 3 * D_MODEL + FOX_HEADS + D_MODEL

kernel_name = "gla_fox_interleaved_hybrid"


def rms_norm(x, g):
    xf = x.astype(jnp.float32)
    y = xf * lax.rsqrt(jnp.mean(xf * xf, axis=-1, keepdims=True) + EPS)
    return (y * g.astype(jnp.float32)).astype(x.dtype)


def sqrelu_mlp(h, w1, w2):
    return jnp.square(jax.nn.relu(h @ w1)) @ w2


def gla_mixer(h, w_in, w_gk_up, b_gk, g_onorm, w_out):
    B, S, _ = h.shape
    C = GLA_CHUNK
    N = S // C
    proj = h @ w_in
    q, k, v, r, gk_low = jnp.split(
        proj, [GLA_DK, 2 * GLA_DK, 2 * GLA_DK + GLA_DV, 2 * GLA_DK + 2 * GLA_DV], axis=-1)
    log_a = jax.nn.log_sigmoid((gk_low @ w_gk_up + b_gk).astype(jnp.float32)) / GLA_GATE_NORM

    def to_chunks(t, dh):
        return t.astype(jnp.float32).reshape(B, N, C, GLA_HEADS, dh).transpose(1, 0, 3, 2, 4)

    qc = to_chunks(q, GLA_HEAD_K) * (GLA_HEAD_K ** -0.5)
    kc = to_chunks(k, GLA_HEAD_K)
    vc = to_chunks(v, GLA_HEAD_V)
    gc = to_chunks(log_a, GLA_HEAD_K)
    causal = jnp.tril(jnp.ones((C, C), dtype=bool))

    def step(state, inp):
        qb, kb, vb, gb = inp
        b = jnp.cumsum(gb, axis=-2)
        o_inter = jnp.einsum('bhck,bhkv->bhcv', qb * jnp.exp(b), state)
        diff = b[:, :, :, None, :] - b[:, :, None, :, :]
        decay = jnp.exp(jnp.where(causal[:, :, None], diff, -jnp.inf))
        attn = jnp.einsum('bhik,bhjk,bhijk->bhij', qb, kb, decay)
        o_intra = jnp.einsum('bhij,bhjv->bhiv', attn, vb)
        b_last = b[:, :, -1:, :]
        k_dec = kb * jnp.exp(b_last - b)
        state = state * jnp.exp(b_last[:, :, 0, :])[..., None] + jnp.einsum('bhck,bhcv->bhkv', k_dec, vb)
        return state, o_inter + o_intra

    state0 = jnp.zeros((B, GLA_HEADS, GLA_HEAD_K, GLA_HEAD_V), jnp.float32)
    _, o = lax.scan(step, state0, (qc, kc, vc, gc))
    o = o.transpose(1, 0, 3, 2, 4).reshape(B, S, GLA_HEADS, GLA_HEAD_V).astype(h.dtype)
    o = rms_norm(o, g_onorm).reshape(B, S, GLA_DV)
    return (o * jax.nn.silu(r)) @ w_out


def fox_mixer(h, w_in, b_f, g_q, g_k, w_out):
    B, S, _ = h.shape
    H, Dh = FOX_HEADS, FOX_HEAD_DIM
    nb = S // FOX_BLOCK
    proj = h @ w_in
    q, k, v, f_logit, o_gate = jnp.split(
        proj, [D_MODEL, 2 * D_MODEL, 3 * D_MODEL, 3 * D_MODEL + H], axis=-1)
    q = rms_norm(q.reshape(B, S, H, Dh), g_q).transpose(0, 2, 1, 3)
    k = rms_norm(k.reshape(B, S, H, Dh), g_k).transpose(0, 2, 1, 3)
    v = v.reshape(B, S, H, Dh).transpose(0, 2, 1, 3)
    log_f = jax.nn.log_sigmoid((f_logit + b_f).astype(jnp.float32))
    c = jnp.cumsum(log_f, axis=1).transpose(0, 2, 1)
    scale = Dh ** -0.5
    key_pos = jnp.arange(S)
    q_blocks = q.reshape(B, H, nb, FOX_BLOCK, Dh).transpose(2, 0, 1, 3, 4)
    c_blocks = c.reshape(B, H, nb, FOX_BLOCK).transpose(2, 0, 1, 3)

    def attend(args):
        qb, cq, blk = args
        s = jnp.einsum('bhqd,bhkd->bhqk', qb, k).astype(jnp.float32) * scale
        s = s + cq[..., None] - c[:, :, None, :]
        q_pos = blk * FOX_BLOCK + jnp.arange(FOX_BLOCK)
        s = jnp.where(key_pos[None, :] <= q_pos[:, None], s, -jnp.inf)
        p = jax.nn.softmax(s, axis=-1)
        return jnp.einsum('bhqk,bhkd->bhqd', p.astype(v.dtype), v)

    o = lax.map(attend, (q_blocks, c_blocks, jnp.arange(nb)))
    o = o.transpose(1, 0, 3, 2, 4).reshape(B, S, D_MODEL)
    return (o * jax.nn.sigmoid(o_gate)) @ w_out


def setup_inputs(seed: int = 0) -> dict:
    key = jax.random.key(seed)
    ks = jax.random.split(key, 24)
    f32 = jnp.float32
    nrm = lambda k, shape, fan_in: jax.random.normal(k, shape, f32) * (fan_in ** -0.5)
    gain = lambda k, n: 1.0 + 0.05 * jax.random.normal(k, (n,), f32)
    return {
        "x": jax.random.normal(ks[0], (BATCH, SEQ, D_MODEL), f32),
        "l0_norm_mix": gain(ks[1], D_MODEL),
        "l0_w_in": nrm(ks[2], (D_MODEL, GLA_IN), D_MODEL),
        "l0_w_gk_up": nrm(ks[3], (GLA_GATE_RANK, GLA_DK), GLA_GATE_RANK),
        "l0_b_gk": 0.1 * jax.random.normal(ks[4], (GLA_DK,), f32),
        "l0_g_onorm": gain(ks[5], GLA_HEAD_V),
        "l0_w_out": nrm(ks[6], (GLA_DV, D_MODEL), GLA_DV),
        "l0_norm_ffn": gain(ks[7], D_MODEL),
        "l0_w_ff1": nrm(ks[8], (D_MODEL, D_FF), D_MODEL),
        "l0_w_ff2": nrm(ks[9], (D_FF, D_MODEL), D_FF),
        "l1_norm_mix": gain(ks[10], D_MODEL),
        "l1_w_in": nrm(ks[11], (D_MODEL, FOX_IN), D_MODEL),
        "l1_b_f": jax.random.uniform(ks[12], (FOX_HEADS,), f32, 1.0, 5.0),
        "l1_g_q": gain(ks[13], FOX_HEAD_DIM),
        "l1_g_k": gain(ks[14], FOX_HEAD_DIM),
        "l1_w_out": nrm(ks[15], (D_MODEL, D_MODEL), D_MODEL),
        "l1_norm_ffn": gain(ks[16], D_MODEL),
        "l1_w_ff1": nrm(ks[17], (D_MODEL, D_FF), D_MODEL),
        "l1_w_ff2": nrm(ks[18], (D_FF, D_MODEL), D_FF),
        "final_norm": gain(ks[19], D_MODEL),
    }


def reference(x, l0_norm_mix, l0_w_in, l0_w_gk_up, l0_b_gk, l0_g_onorm, l0_w_out,
              l0_norm_ffn, l0_w_ff1, l0_w_ff2,
              l1_norm_mix, l1_w_in, l1_b_f, l1_g_q, l1_g_k, l1_w_out,
              l1_norm_ffn, l1_w_ff1, l1_w_ff2, final_norm):
    layers = (
        (l0_norm_mix, (l0_w_in, l0_w_gk_up, l0_b_gk, l0_g_onorm, l0_w_out), l0_norm_ffn, l0_w_ff1, l0_w_ff2),
        (l1_norm_mix, (l1_w_in, l1_b_f, l1_g_q, l1_g_k, l1_w_out), l1_norm_ffn, l1_w_ff1, l1_w_ff2),
    )
    mixers = (gla_mixer, fox_mixer)
    for i in range(DEPTH):
        norm_mix, mix_params, norm_ffn, w1, w2 = layers[i]
        x = x + mixers[i % N_MIXERS](rms_norm(x, norm_mix), *mix_params)
        x = x + sqrelu_mlp(rms_norm(x, norm_ffn), w1, w2)
    return rms_norm(x, final_norm)
```

```python
import os
import numpy as np
from contextlib import ExitStack
import concourse.bass as bass
import concourse.mybir as mybir
from concourse.bass_utils import run_bass_kernel_spmd

F32 = mybir.dt.float32
BF16 = mybir.dt.bfloat16
AF = mybir.ActivationFunctionType
ALU = mybir.AluOpType
AX = mybir.AxisListType
ENGS = ["pe", "act", "dve", "pool", "sp"]
D = 1024
EPS = 1e-6


class Op:
    __slots__ = ("eng", "fn", "deps", "alldeps", "signal", "dma_key", "dma_cnt", "sigcount", "cost", "idx", "bytes",
                 "succ", "nin", "rt", "fin")


LAT_X = 0.15
LAT_S = 0.05


class Sched:
    def __init__(self, nc):
        self.nc = nc
        self.ops = {e: [] for e in ENGS}
        self.last_w = {}
        self.readers = {}
        self.dma_cnt = {}
        self.dma_last = {}
        self.sems = {}
        self.n = 0

    def add(self, eng, fn, reads=(), writes=(), dma_key=None, cost=0.2, nbytes=0):
        op = Op()
        op.eng = eng
        op.fn = fn
        op.signal = False
        op.dma_key = dma_key
        op.dma_cnt = 0
        op.sigcount = 0
        op.cost = cost
        op.bytes = nbytes
        op.idx = self.n
        self.n += 1
        deps = []
        order_only = []
        for u in reads:
            w = self.last_w.get(u)
            if w is not None:
                deps.append(w)
        for u in writes:
            w = self.last_w.get(u)
            if w is not None:
                if dma_key is not None and w.dma_key == dma_key and eng != "pool":
                    deps.extend(w.deps)
                    order_only.append(w)
                else:
                    deps.append(w)
            deps.extend(self.readers.get(u, ()))
        dd = []
        ad = []
        seen = set()
        for d in deps:
            if id(d) in seen or d is op:
                continue
            seen.add(id(d))
            ad.append(d)
            if d.dma_key is None and d.eng == "pe" and eng == "pe" and dma_key is None:
                continue
            dd.append(d)
            if d.dma_key is None:
                d.signal = True
        for d in order_only:
            if id(d) not in seen:
                seen.add(id(d))
                ad.append(d)
        op.deps = dd
        op.alldeps = ad
        for u in reads:
            self.readers.setdefault(u, []).append(op)
        for u in writes:
            self.last_w[u] = op
            self.readers[u] = []
        if dma_key is not None:
            self.dma_cnt[dma_key] = self.dma_cnt.get(dma_key, 0) + 1
            op.dma_cnt = self.dma_cnt[dma_key]
            self.dma_last[dma_key] = op
        self.ops[eng].append(op)
        return op

    def barrier(self):
        dl = list(self.dma_last.values())
        for e in ENGS:
            op = Op()
            op.eng = e
            op.fn = None
            op.signal = False
            op.dma_key = None
            op.dma_cnt = 0
            op.sigcount = 0
            op.cost = 0.0
            op.bytes = 0
            op.idx = self.n
            op.deps = dl
            op.alldeps = []
            self.ops[e].append(op)
        self.n += 1
        self.last_w = {}
        self.readers = {}

    def _fill_barriers(self):
        pos = {e: 0 for e in ENGS}
        while True:
            lasts = []
            bars = []
            done = False
            for e in ENGS:
                lst = self.ops[e]
                j = pos[e]
                last = None
                while j < len(lst) and lst[j].fn is not None:
                    if lst[j].dma_key is None:
                        last = lst[j]
                    j += 1
                if j >= len(lst):
                    done = True
                    break
                bars.append(lst[j])
                k = j - 1
                while last is None and k >= 0:
                    if lst[k].fn is not None and lst[k].dma_key is None:
                        last = lst[k]
                    k -= 1
                if last is not None:
                    lasts.append(last)
                pos[e] = j + 1
            if done:
                break
            for l_ in lasts:
                l_.signal = True
            for bop in bars:
                bop.deps = list(bop.deps) + lasts

    def check(self):
        cnt = {}
        pos = {e: 0 for e in ENGS}
        total = sum(len(v) for v in self.ops.values())
        donec = 0
        progress = True
        while progress:
            progress = False
            for e in ENGS:
                lst = self.ops[e]
                while pos[e] < len(lst):
                    op = lst[pos[e]]
                    ok = True
                    for d in op.deps:
                        if d.dma_key is not None:
                            if cnt.get(("dma", d.dma_key), 0) < d.dma_cnt * 16:
                                ok = False
                                break
                        else:
                            if cnt.get(("eng", d.eng), 0) < d.sigcount:
                                ok = False
                                break
                    if not ok:
                        break
                    if op.fn is not None:
                        if op.dma_key is not None:
                            k = ("dma", op.dma_key)
                            cnt[k] = cnt.get(k, 0) + 16
                        elif op.signal:
                            k = ("eng", op.eng)
                            cnt[k] = cnt.get(k, 0) + 1
                    pos[e] += 1
                    donec += 1
                    progress = True
        if donec != total:
            msg = []
            for e in ENGS:
                if pos[e] < len(self.ops[e]):
                    op = self.ops[e][pos[e]]
                    msg.append((e, pos[e], len(self.ops[e]), op.idx, [(d.eng, d.dma_key, d.dma_cnt, d.sigcount, d.idx) for d in op.deps][:6]))
            raise RuntimeError("DEADLOCK in emitted program: %s" % (msg,))

    def _schedule_segment(self, seg):
        import heapq
        allops = []
        for e in ENGS:
            allops.extend(seg[e])
        inseg = set(id(o) for o in allops)
        for o in allops:
            o.succ = []
            o.nin = 0
            o.rt = 0.0
            o.fin = 0.0
        for o in allops:
            for d in o.alldeps:
                if id(d) in inseg:
                    d.succ.append(o)
                    o.nin += 1
        avail = {e: [] for e in ENGS}
        future = {e: [] for e in ENGS}
        free = {e: 0.0 for e in ENGS}
        dma_free = [0.0]
        out = {e: [] for e in ENGS}
        for o in allops:
            if o.nin == 0:
                heapq.heappush(future[o.eng], (0.0, o.idx, o))
        remaining = len(allops)
        while remaining:
            best = None
            for e in ENGS:
                fu = future[e]
                av = avail[e]
                while fu and fu[0][0] <= free[e]:
                    r, i, o = heapq.heappop(fu)
                    heapq.heappush(av, (i, o))
                if av:
                    st = free[e]
                elif fu:
                    st = fu[0][0]
                else:
                    continue
                if best is None or st < best[0]:
                    best = (st, e)
            st, e = best
            if avail[e]:
                i, o = heapq.heappop(avail[e])
            else:
                r, i, o = heapq.heappop(future[e])
            if o.dma_key is not None:
                xs = max(st, dma_free[0])
                dur = o.bytes / 1.6e5
                dma_free[0] = xs + dur
                o.fin = xs + dur + 2.0
                free[e] = st + (0.6 if e == "pool" else 0.12)
            else:
                o.fin = st + o.cost
                free[e] = o.fin
            out[e].append(o)
            remaining -= 1
            for s_ in o.succ:
                lat = 0.0 if (s_.eng == "pe" and o.eng == "pe" and o.dma_key is None) else (LAT_S if s_.eng == o.eng and o.dma_key is None else LAT_X)
                t = o.fin + lat
                if t > s_.rt:
                    s_.rt = t
                s_.nin -= 1
                if s_.nin == 0:
                    heapq.heappush(future[s_.eng], (s_.rt, s_.idx, s_))
        return out, max([0.0] + [o.fin for o in allops])

    def schedule(self):
        nseg = sum(1 for o in self.ops["pe"] if o.fn is None)
        cur = {e: 0 for e in ENGS}
        newops = {e: [] for e in ENGS}
        total = 0.0
        while True:
            seg = {}
            bar = {}
            for e in ENGS:
                lst = self.ops[e]
                i = cur[e]
                j = i
                while j < len(lst) and lst[j].fn is not None:
                    j += 1
                seg[e] = lst[i:j]
                bar[e] = lst[j] if j < len(lst) else None
                cur[e] = j + 1
            if any(seg[e] for e in ENGS):
                out, t = self._schedule_segment(seg)
                total += t
                self.seg_times = getattr(self, "seg_times", []) + [t]
            else:
                out = seg
            for e in ENGS:
                newops[e].extend(out[e])
                if bar[e] is not None:
                    newops[e].append(bar[e])
            if all(cur[e] >= len(self.ops[e]) for e in ENGS):
                break
        self.ops = newops
        self.est_total = total

    def run(self, stack, reorder=True):
        if reorder:
            self.schedule()
        self._fill_barriers()
        for e in ENGS:
            c = 0
            for o in self.ops[e]:
                if o.dma_key is None and o.signal and o.fn is not None:
                    c += 1
                o.sigcount = c
        self.check()
        keys = [("eng", e) for e in ENGS] + [("dma", k) for k in self.dma_cnt]
        for k in keys:
            self.sems[k] = stack.enter_context(self.nc.semaphore("s_" + "_".join(str(x) for x in k)))
        blk = stack.enter_context(self.nc.Block())

        @blk.tensor
        def _(e):
            self.emit("pe", e)

        @blk.scalar
        def _(e):
            self.emit("act", e)

        @blk.vector
        def _(e):
            self.emit("dve", e)

        @blk.gpsimd
        def _(e):
            self.emit("pool", e)

        @blk.sync
        def _(e):
            self.emit("sp", e)

    def emit(self, eng_name, eng):
        waited = {}
        for op in self.ops[eng_name]:
            need = {}
            for d in op.deps:
                if d.dma_key is not None:
                    k = ("dma", d.dma_key)
                    v = d.dma_cnt * 16
                else:
                    k = ("eng", d.eng)
                    v = d.sigcount
                if need.get(k, 0) < v:
                    need[k] = v
            for k, v in need.items():
                if waited.get(k, 0) < v:
                    eng.wait_ge(self.sems[k], v)
                    waited[k] = v
            if op.fn is None:
                continue
            inst = op.fn(eng)
            if op.dma_key is not None:
                inst.then_inc(self.sems[("dma", op.dma_key)], 16)
            elif op.signal:
                inst.then_inc(self.sems[("eng", op.eng)], 1)


C_ID, C_TU16, C_TL16, C_TUM1, C_E, C_M01, C_MNEG, C_COL = 0, 128, 256, 384, 512, 640, 768, 896
NCST = 908


def make_consts():
    c = np.zeros((128, NCST), np.float32)
    s = np.arange(128)[:, None]
    t = np.arange(128)[None, :]
    c[:, C_ID:C_ID + 128] = (s == t)
    c[:, C_TU16:C_TU16 + 128] = np.where(s <= t, -1.0 / 16.0, 0.0)
    c[:, C_TL16:C_TL16 + 128] = np.where(s > t, -1.0 / 16.0, 0.0)
    c[:, C_TUM1:C_TUM1 + 128] = np.where(s <= t, -1.0, 0.0)
    c[:, C_E:C_E + 128] = (s == 127)
    c[:, C_M01:C_M01 + 128] = (t >= s)
    c[:, C_MNEG:C_MNEG + 128] = np.where(t >= s, 0.0, -30000.0)
    c[:, C_COL + 0] = EPS
    c[:, C_COL + 1] = 1.0
    c[:, C_COL + 2] = -1.0 / 16.0
    c[:, C_COL + 3] = 0.0
    p = np.arange(128)
    c[:, C_COL + 4] = np.where((p >= 67) & (p <= 69), -1.0, 0.0)
    c[:, C_COL + 5] = (p == 70)
    c[:, C_COL + 6] = (p == 71)
    c[:, C_COL + 7] = (p != 71)
    return c


def fsz(ap):
    return int(np.prod(ap.shape[1:]))


AW = 16912


class Bld:
    def __init__(self, nc, st, T):
        self.nc = nc
        self.st = st
        self.T = T
        self.S = Sched(nc)
        self.apos = 0
        self.bank_i = 0
        self.bank_i1 = 0
        self.bank_i2 = 0

    def sb(self, name, shape, dt):
        return self.st.enter_context(self.nc.sbuf_tensor(name, shape, dt))

    def areset(self):
        self.apos = 0

    def af(self, n):
        off = self.apos
        self.apos += n
        assert self.apos <= AW, ("arena overflow", self.apos)
        return self.ar[:, off:off + n]

    def ab(self, n):
        return self.af((n + 1) // 2).bitcast(BF16)

    def af1(self, n):
        off = self.r1pos
        self.r1pos += n
        assert self.r1pos <= 16384
        return self.R1.bitcast(F32)[:, off:off + n]

    def ab1(self, n):
        return self.af1((n + 1) // 2).bitcast(BF16)

    def bank(self, pool=0):
        if pool == 0:
            i = self.bank_i % 8
            self.bank_i += 1
        elif pool == 1:
            i = self.bank_i1 % 4
            self.bank_i1 += 1
        else:
            i = 4 + self.bank_i2 % 4
            self.bank_i2 += 1
        return self.banks[i], ("ps", i)

    def mm(self, out, lhsT, rhs, start, stop, r, w):
        n = fsz(rhs)
        c = (4.0 * max(n, 96) / 1950.0 + 0.05) if lhsT.dtype == F32 else (max(n, 96) / 1950.0 + 0.005)
        self.S.add("pe", lambda e: e.matmul(out, lhsT=lhsT, rhs=rhs, start=start, stop=stop), reads=r, writes=w, cost=c)

    def tp(self, out, in_, r, w):
        ident = self.ident
        self.S.add("pe", lambda e: e.transpose(out, in_, ident), reads=r, writes=w, cost=0.075)

    def act(self, out, in_, func, r, w, bias=None, scale=None, accum=None):
        kw = {}
        if bias is not None:
            kw["bias"] = bias
        if scale is not None:
            kw["scale"] = scale
        if accum is not None:
            kw["accum_out"] = accum
        c = 0.13 + fsz(in_) / 1400.0 + (0.1 if accum is not None else 0.0) + (0.09 if bias is not None and not isinstance(bias, float) else 0.0)
        self.S.add("act", lambda e: e.activation(out, in_, func, **kw), reads=r, writes=w, cost=c)

    def tt(self, eng, out, in0, in1, op, r, w):
        self.S.add(eng, lambda e: e.tensor_tensor(out, in0, in1, op), reads=r, writes=w, cost=0.07 + fsz(in0) / 960.0)

    def stt(self, out, in0, scalar, in1, op0, op1, r, w):
        self.S.add("dve", lambda e: e.scalar_tensor_tensor(out, in0, scalar, in1, op0, op1), reads=r, writes=w,
                   cost=0.07 + fsz(in0) / 960.0)

    def ts(self, eng, out, in0, s1, s2, op0, op1, r, w):
        self.S.add(eng, lambda e: e.tensor_scalar(out, in0, s1, s2, op0, op1), reads=r, writes=w, cost=0.07 + fsz(in0) / 960.0)

    def cp(self, eng, out, in_, r, w):
        if eng == "act":
            self.S.add("act", lambda e: e.activation(out, in_, AF.Copy), reads=r, writes=w, cost=0.22 + fsz(in_) / 1400.0)
        else:
            self.S.add(eng, lambda e: e.tensor_copy(out, in_), reads=r, writes=w, cost=0.07 + fsz(in_) / 960.0)

    def memset(self, eng, ap, val, w):
        self.S.add(eng, lambda e: e.memset(ap, val), writes=w, cost=0.12 + fsz(ap) / 960.0)

    def dma(self, q, out, in_, key, r=(), w=()):
        nb = int(np.prod(out.shape)) * (2 if out.dtype == BF16 else 4)
        self.S.add(q, lambda e: e.dma_start(out=out, in_=in_), reads=r, writes=w, dma_key=key, nbytes=nb)

    def load_w(self, dst3, src2d, C, N, key, unit, rearr=False):
        i = 0
        for c in range(C):
            for a in range(0, N, 2048):
                b_ = min(N, a + 2048)
                sfx = "b" if i % 2 else ""
                self.dma("pool", dst3[:, c, a:b_], src2d[c * 128:(c + 1) * 128, a:b_], key + sfx, w=[unit + sfx])
                i += 1


def norm_block(b, xt, ux, gv, ugv, hb, uhb, tmp):
    sf = tmp.get("sfx", "")
    b.act(tmp["junk"], xt, AF.Square, r=[ux], w=["ss" + sf, tmp["ujunk"]], accum=tmp["ss"])
    b.act(tmp["lnv"], tmp["ss"], AF.Ln, r=["ss" + sf], w=["lnv" + sf], bias=b.c_eps, scale=1.0 / D)
    b.act(tmp["rstd"], tmp["lnv"], AF.Exp, r=["lnv" + sf], w=["rstd" + sf], scale=-0.5)
    b.stt(hb, xt, tmp["rstd"], gv, ALU.mult, ALU.mult, r=[ux, "rstd" + sf, ugv], w=[uhb])


def transpose_to(b, src, usrc, dst3, udst, cp_eng, pool=0):
    pb, up = b.bank(pool)
    pv = pb.bitcast(BF16).rearrange("p (c n) -> p c n", c=8)
    for c in range(8):
        b.tp(pv[:, c, :], src[:, c * 128:(c + 1) * 128], r=[usrc], w=[up])
    b.cp(cp_eng, dst3, pv, r=[up], w=[udst])


def load_gvec(b, dram_ap, dst, unit, key="gv"):
    b.dma("sp", dst, dram_ap, key, w=[unit])


def phase_mlp(b, xin, xout, w1d, w2d, gd, final_gd=None, w1_loaded=False, QS=0, out_off=0):
    T = b.T
    NT = T // 512
    b.areset()
    w1 = b.R1.rearrange("p (c n) -> p c n", c=8)
    w2 = b.R2[:, 0:32 * 1024].rearrange("p (f n) -> p f n", f=32)
    if not w1_loaded:
        b.load_w(w1, w1d, 8, 4096, "R1", "R1")
    for f0 in range(0, 32, 4):
        sfx = "b" if (f0 // 4) % 2 else ""
        b.dma("pool", w2[:, f0:f0 + 4, :], w2d[f0 * 128:(f0 + 4) * 128, :].rearrange("(f p) n -> p f n", p=128), "R2" + sfx, w=["R2" + sfx])
    load_gvec(b, gd, b.gv, "gv")
    gv2 = None
    if final_gd is not None:
        gv2 = b.af(1024)
        load_gvec(b, final_gd, gv2, "gv2", key="gv2")
    xt = b.af(4 * 1024).rearrange("p (k n) -> p k n", k=4)
    hb1 = b.ab(1024)
    hbs = [hb1, hb1]
    hT = b.ab(8 * 512).rearrange("p (c n) -> p c n", c=8)
    uT = b.ab(32 * 512).rearrange("p (f n) -> p f n", f=32)
    rts = [b.af(512), b.af(512)]
    sm = b.af(8)
    tmp = {"ss": sm[:, 0:1], "lnv": sm[:, 1:2], "rstd": sm[:, 2:3], "junk": rts[0].bitcast(BF16), "ujunk": ("rt", 0)}
    for ti in range(QS, NT):
        for k in range(4):
            r0 = (ti * 4 + k) * 128
            b.dma("sp", xt[:, k, :], xin[r0:r0 + 128, :], "xt%d" % k, w=[("xt", k)])
        for k in range(4):
            hb = hbs[0]
            uhb = ("hb", 0)
            norm_block(b, xt[:, k, :], ("xt", k), b.gv, "gv", hb, uhb, tmp)
            transpose_to(b, hb, uhb, hT[:, :, k * 128:(k + 1) * 128], ("hT", k), "dve" if k % 2 else "act")
        hTu = [("hT", k) for k in range(4)]
        for f in range(32):
            pb, up = b.bank()
            for c in range(8):
                b.mm(pb, w1[:, c, f * 128:(f + 1) * 128], hT[:, c, :], c == 0, c == 7, r=["R1", "R1b"] + hTu, w=[up])
            rt = rts[f % 2]
            b.act(rt, pb, AF.Relu, r=[up], w=[("rt", f % 2)])
            b.tt("pool" if f % 2 else "dve", uT[:, f, :], rt, rt, ALU.mult, r=[("rt", f % 2)], w=[("uT", f)])
        uTu = [("uT", f) for f in range(32)]
        for k in range(4):
            for n2 in range(2):
                pb, up = b.bank()
                for f in range(32):
                    b.mm(pb, uT[:, f, k * 128:(k + 1) * 128], w2[:, f, n2 * 512:(n2 + 1) * 512], f == 0, f == 31,
                         r=["R2", "R2b"] + uTu, w=[up])
                xs = xt[:, k, n2 * 512:(n2 + 1) * 512]
                b.tt("dve", xs, xs, pb, ALU.add, r=[up, ("xt", k)], w=[("xt", k)])
            r0 = (ti * 4 + k) * 128
            ro = r0 - out_off
            if final_gd is None:
                b.dma("sp", xout[ro:ro + 128, :], xt[:, k, :], "xt%d" % k, r=[("xt", k)])
            else:
                b.act(tmp["junk"], xt[:, k, :], AF.Square, r=[("xt", k)], w=["ss", tmp["ujunk"]], accum=tmp["ss"])
                b.act(tmp["lnv"], tmp["ss"], AF.Ln, r=["ss"], w=["lnv"], bias=b.c_eps, scale=1.0 / D)
                b.act(tmp["rstd"], tmp["lnv"], AF.Exp, r=["lnv"], w=["rstd"], scale=-0.5)
                b.stt(xt[:, k, :], xt[:, k, :], tmp["rstd"], gv2, ALU.mult, ALU.mult, r=[("xt", k), "rstd", "gv2"], w=[("xt", k)])
                b.dma("sp", xout[ro:ro + 128, :], xt[:, k, :], "xt%d" % k, r=[("xt", k)])
    b.S.barrier()


def phase_gla(b, xin, xout, wind, wgkd, wod, gd, gond, w1d_prefetch=None):
    T = b.T
    NB = T // 128
    b.areset()
    NIN = 3088
    win = b.R2[:, 0:8 * NIN].rearrange("p (c n) -> p c n", c=8)
    wo = b.R2[:, 8 * NIN:8 * NIN + 8 * 1024].rearrange("p (c n) -> p c n", c=8)
    b.load_w(win, wind, 8, NIN, "R2", "R2")
    b.load_w(wo, wod, 8, 1024, "R2", "R2")
    load_gvec(b, gd, b.gv, "gv")
    gv2 = b.af(1024)
    load_gvec(b, gond, gv2, "gv2", key="gv2")
    if w1d_prefetch is not None:
        b.load_w(b.R1.rearrange("p (c n) -> p c n", c=8), w1d_prefetch, 8, 4096, "R1", "R1")
    wgk = b.af(512)
    b.dma("sp", wgk[0:17, :], wgkd, "wgk", w=["wgk"])
    gkT = b.af(128)
    b.memset("dve", gkT[0:32, :], 1.0, w=[("gkT", 0)])
    Sst = b.af(1024).rearrange("p (h n) -> p h n", h=4)
    Sbf = b.ab(1024).rearrange("p (h n) -> p h n", h=4)
    b.memset("dve", Sst, 0.0, w=["S"])
    b.memset("pool", Sbf, 0.0, w=["Sbf"])
    b.r1pos = 0
    xts = [b.af(1024), b.af(1024)]
    hbs = [b.ab(1024), b.ab1(1024)]
    hTs = [b.ab(1024).rearrange("p (c n) -> p c n", c=8), b.ab1(1024).rearrange("p (c n) -> p c n", c=8)]
    ets = [b.af(512), b.af1(512)]
    lses = [b.af(512), b.af1(512)]
    ebs = [b.af(512), b.af1(512)]
    enbs = [b.af(512), b.af1(512)]
    ebrs = [b.af(512), b.af1(512)]
    decs = b.af(8)
    qds = [b.ab(512), b.ab1(512)]
    kis = [b.ab(512), b.ab1(512)]
    kds = [b.ab(512), b.ab(512)]
    qkTs = [b.ab(1024).rearrange("p (c n) -> p c n", c=8) for _ in range(2)]
    ATs = [b.ab(512).rearrange("p (h n) -> p h n", h=4), b.ab1(512).rearrange("p (h n) -> p h n", h=4)]
    vbfs = [b.ab(1024), b.ab(1024)]
    gates = [b.af(1024), b.af1(1024)]
    gws = [b.af(1024), b.af(1024)]
    junks = [b.ab(1024), b.ab1(1024)]
    ogs = [b.ab(1024), b.ab1(1024)]
    ogTs = [b.ab(1024).rearrange("p (c n) -> p c n", c=8), b.ab1(1024).rearrange("p (c n) -> p c n", c=8)]
    gkTs = [gkT, b.af1(128)]
    b.memset("dve", gkTs[1][0:32, :], 1.0, w=[("gkT", 1)])
    sms = [b.af(16), b.af1(16)]
    mask01 = b.mask01
    for n in range(NB):
        r0 = n * 128
        par = n % 2

        def U(name, _p=par):
            return (name, _p)
        xt, uxt = xts[par], "xt%d" % par
        kd, ukd = kds[par], "kd%d" % par
        qkT, uqk = qkTs[par], "qkT%d" % par
        vbf, uvb = vbfs[par], "vbf%d" % par
        gw, ugw = gws[par], "gw%d" % par
        dec, udec = decs[:, par * 4:par * 4 + 4], "dec%d" % par
        hb, hT, et, lse, eb, enb, ebr = hbs[par], hTs[par], ets[par], lses[par], ebs[par], enbs[par], ebrs[par]
        qd, ki, AT, gate, junk, og, ogT, gkT, sm = qds[par], kis[par], ATs[par], gates[par], junks[par], ogs[par], ogTs[par], gkTs[par], sms[par]
        junk4 = junk
        tmp = {"ss": sm[:, 0:1], "lnv": sm[:, 1:2], "rstd": sm[:, 2:3], "junk": junk, "ujunk": U("junk"), "sfx": "_%d" % par}
        ss4 = sm[:, 4:8]
        ln4 = sm[:, 8:12]
        rs4 = sm[:, 12:16]
        b.dma("sp", xt, xin[r0:r0 + 128, :], uxt, w=[uxt])
        norm_block(b, xt, uxt, b.gv, "gv", hb, U("hb"), tmp)
        transpose_to(b, hb, U("hb"), hT, U("hT"), "act")
        pb, up = b.bank()
        for c in range(8):
            b.mm(pb[0:16, 0:128], win[:, c, 3072:3088], hT[:, c, :], c == 0, c == 7, r=["R2", "R2b", U("hT")], w=[up])
        b.cp("act", gkT[0:16, :], pb[0:16, 0:128], r=[up], w=[U("gkT")])
        pg, upg = b.bank()
        b.mm(pg, gkT[0:17, :], wgk[0:17, :], True, True, r=[U("gkT"), "wgk"], w=[upg])
        b.act(et, pg, AF.Exp, r=[upg], w=[U("et")], scale=-1.0)
        b.act(lse, et, AF.Ln, r=[U("et")], w=[U("lse")], bias=b.c_one, scale=1.0)
        pbb, upb = b.bank()
        b.mm(pbb, b.tu16, lse, True, True, r=[U("lse")], w=[upb])
        pbl, upl = b.bank()
        for h in range(4):
            b.mm(pbl[:, 2 * h:2 * h + 2], lse[:, h * 128:(h + 1) * 128], b.m16col2, True, True, r=[U("lse")], w=[upl])
        b.act(eb, pbb, AF.Exp, r=[upb], w=[U("eb")])
        b.act(enb, pbb, AF.Exp, r=[upb], w=[U("enb")], scale=-1.0)
        b.act(dec, pbl.rearrange("p (h two) -> p h two", two=2)[:, 0:4, 0], AF.Exp, r=[upl], w=[udec])
        pq, upq = b.bank()
        for c in range(8):
            b.mm(pq, hT[:, c, :], win[:, c, 0:512], c == 0, c == 7, r=["R2", "R2b", U("hT")], w=[upq])
        pk, upk = b.bank()
        for c in range(8):
            b.mm(pk, hT[:, c, :], win[:, c, 512:1024], c == 0, c == 7, r=["R2", "R2b", U("hT")], w=[upk])
        b.stt(qd, pq, float(128 ** -0.5), eb, ALU.mult, ALU.mult, r=[upq, U("eb")], w=[U("qd")])
        b.tt("dve", ki, pk, enb, ALU.mult, r=[upk, U("enb")], w=[U("ki")])
        for j in range(2):
            pv, upv = b.bank()
            for c in range(8):
                b.mm(pv, hT[:, c, :], win[:, c, 1024 + j * 512:1536 + j * 512], c == 0, c == 7, r=["R2", "R2b", U("hT")], w=[upv])
            b.cp("act", vbf[:, j * 512:(j + 1) * 512], pv, r=[upv], w=[(uvb, j)])
        for j in range(2):
            pr, upr2 = b.bank()
            for c in range(8):
                b.mm(pr, hT[:, c, :], win[:, c, 2048 + j * 512:2560 + j * 512], c == 0, c == 7, r=["R2", "R2b", U("hT")], w=[upr2])
            b.act(gate[:, j * 512:(j + 1) * 512], pr, AF.Silu, r=[upr2], w=[U(("gate", j))])
            b.tt("pool", gw[:, j * 512:(j + 1) * 512], gate[:, j * 512:(j + 1) * 512], gv2[:, j * 512:(j + 1) * 512],
                 ALU.mult, r=[U(("gate", j)), "gv2"], w=[(ugw, j)])
        pt, upt = b.bank()
        ptv = pt.bitcast(BF16).rearrange("p (c n) -> p c n", c=8)
        for h in range(4):
            b.tp(ptv[:, h, :], qd[:, h * 128:(h + 1) * 128], r=[U("qd")], w=[upt])
        for h in range(4):
            b.tp(ptv[:, 4 + h, :], ki[:, h * 128:(h + 1) * 128], r=[U("ki")], w=[upt])
        b.cp("dve", qkT, ptv, r=[upt], w=[uqk])
        pa, upa = b.bank()
        for h in range(4):
            b.mm(pa[:, h * 128:(h + 1) * 128], qkT[:, 4 + h, :], qkT[:, h, :], True, True, r=[uqk], w=[upa])
        for h in range(4):
            b.tt("dve", AT[:, h, :], pa[:, h * 128:(h + 1) * 128], mask01, ALU.mult, r=[upa], w=[U(("AT", h))])
        po = [b.bank(), b.bank()]
        for h in range(4):
            pob, upo = po[h // 2]
            cs_ = slice((h % 2) * 256, (h % 2) * 256 + 256)
            b.mm(pob[:, cs_], qkT[:, h, :], Sbf[:, h, :], True, False, r=[uqk, "Sbf"], w=[upo])
            b.mm(pob[:, cs_], AT[:, h, :], vbf[:, h * 256:(h + 1) * 256], False, True, r=[U(("AT", h)), (uvb, h // 2)], w=[upo])
        pkv = [b.bank(), b.bank()]
        for h in range(4):
            pkb, upkv = pkv[h // 2]
            cs_ = slice((h % 2) * 256, (h % 2) * 256 + 256)
            b.mm(pkb[:, cs_], ki[:, h * 128:(h + 1) * 128], vbf[:, h * 256:(h + 1) * 256], True, True, r=[U("ki"), (uvb, h // 2)], w=[upkv])
        for h in range(4):
            b.ts("dve", Sst[:, h, :], Sst[:, h, :], dec[:, h:h + 1], None, ALU.mult, ALU.bypass, r=["S", udec], w=["S"])
        for h in range(4):
            pkb, upkv = pkv[h // 2]
            cs_ = slice((h % 2) * 256, (h % 2) * 256 + 256)
            b.stt(Sst[:, h, :], pkb[:, cs_], dec[:, h:h + 1], Sst[:, h, :], ALU.mult, ALU.add, r=["S", udec, upkv], w=["S"])
        b.cp("pool", Sbf, Sst, r=["S"], w=["Sbf"])
        for h in range(4):
            pob, upo = po[h // 2]
            cs_ = slice((h % 2) * 256, (h % 2) * 256 + 256)
            b.act(junk4[:, h * 256:(h + 1) * 256], pob[:, cs_], AF.Square, r=[upo], w=[U(("ss4", h)), U("junk")], accum=ss4[:, h:h + 1])
        b.act(ln4, ss4, AF.Ln, r=[U(("ss4", h)) for h in range(4)], w=[U("ln4")], bias=b.c_eps, scale=1.0 / 256.0)
        b.act(rs4, ln4, AF.Exp, r=[U("ln4")], w=[U("rs4")], scale=-0.5)
        for h in range(4):
            pob, upo = po[h // 2]
            cs_ = slice((h % 2) * 256, (h % 2) * 256 + 256)
            b.stt(og[:, h * 256:(h + 1) * 256], pob[:, cs_], rs4[:, h:h + 1], gw[:, h * 256:(h + 1) * 256], ALU.mult, ALU.mult,
                  r=[upo, U("rs4"), (ugw, h // 2)], w=[U("og")])
        transpose_to(b, og, U("og"), ogT, U("ogT"), "act")
        for n2 in range(2):
            py, upy = b.bank()
            for c in range(8):
                b.mm(py, ogT[:, c, :], wo[:, c, n2 * 512:(n2 + 1) * 512], c == 0, c == 7, r=["R2", "R2b", U("ogT")], w=[upy])
            xs = xt[:, n2 * 512:(n2 + 1) * 512]
            b.tt("dve", xs, xs, py, ALU.add, r=[upy, uxt], w=[uxt])
        b.dma("sp", xout[r0:r0 + 128, :], xt, uxt, r=[uxt])
    b.S.barrier()


def phase_fox_proj(b, xin, wind, gd, gqd, gkd, bfd, scr, NPB=0, pbd=None, QS=0):
    T = b.T
    NT = T // 512
    b.areset()
    NIN = 4112
    win = b.R2[:, 0:8 * NIN].rearrange("p (c n) -> p c n", c=8)
    b.load_w(win, wind, 8, NIN, "R2", "R2")
    load_gvec(b, gd, b.gv, "gv")
    gq8 = b.af(1024)
    gk1 = b.af(1024)
    b.dma("sp", gq8, gqd, "gq", w=["gq8"])
    b.dma("sp", gk1, gkd, "gk", w=["gk1"])
    bfb = b.af(16)
    b.dma("sp", bfb, bfd, "bfb", w=["bfb"])
    mx = b.af(4)
    b.S.add("dve", lambda e: e.tensor_reduce(mx[:, 0:1], gq8[:, 0:64], AX.X, ALU.max, apply_absolute_value=True), reads=["gq8"], writes=["mxq"])
    b.S.add("dve", lambda e: e.tensor_reduce(mx[:, 1:2], gk1[:, 0:64], AX.X, ALU.max, apply_absolute_value=True), reads=["gk1"], writes=["mxk"])
    b.ts("dve", b.Mcol, mx[:, 0:1], mx[:, 1:2], 8.0, ALU.mult, ALU.mult, r=["mxq", "mxk"], w=["Mcol"])
    b.ts("dve", gq8, gq8, 0.125, None, ALU.mult, ALU.bypass, r=["gq8"], w=["gq8"])
    xt = b.af(1024)
    hb = b.ab(1024)
    hT = b.ab(1024).rearrange("p (c n) -> p c n", c=8)
    sq = b.af(1024)
    qtmp = b.af(1024)
    qn = b.ab(1024)
    gsb = b.ab(1024)
    qst = b.ab(8 * 512).rearrange("p (c n) -> p c n", c=8)
    kst = b.ab(8 * 512).rearrange("p (c n) -> p c n", c=8)
    gst = b.ab(8 * 512).rearrange("p (c n) -> p c n", c=8)
    vst = b.ab(16 * 4 * 64).rearrange("p (h k d) -> p h k d", h=16, k=4)
    cst3 = b.ab(512)
    sm = b.af(8)
    junk = b.ab(1024)
    tmp = {"ss": sm[:, 0:1], "lnv": sm[:, 1:2], "rstd": sm[:, 2:3], "junk": junk, "ujunk": "junk"}
    ss16 = b.af(16)
    ln16 = b.af(16)
    rs16 = b.af(16)
    ft = b.af(16)
    fe = b.af(16)
    lsp = b.af(16)
    r1 = b.af(16)
    r2 = b.af(16)
    c3 = b.ab(48)
    zer = b.af(16)
    b.memset("dve", zer, 0.0, w=["zer"])
    cs = b.cs
    for ti in range(NT):
        for k in range(4):
            n = ti * 4 + k
            r0 = n * 128
            b.dma("sp", xt, xin[r0:r0 + 128, :], "xt0", w=["xt"])
            norm_block(b, xt, "xt", b.gv, "gv", hb, "hb", tmp)
            transpose_to(b, hb, "hb", hT, "hT", "act")
            own = ti >= QS
            for which, (c0, gvec, ugv, stg, ustg) in enumerate(((0, gq8, "gq8", qst, "qst"), (1024, gk1, "gk1", kst, "kst"))):
                if which == 0 and not own:
                    continue
                pbs = []
                for j in range(2):
                    pb, up = b.bank()
                    for c in range(8):
                        b.mm(pb, hT[:, c, :], win[:, c, c0 + j * 512:c0 + (j + 1) * 512], c == 0, c == 7, r=["R2", "R2b", "hT"], w=[up])
                    b.act(sq[:, j * 512:(j + 1) * 512], pb, AF.Square, r=[up], w=[("sq", j)])
                    pbs.append((pb, up))
                b.S.add("dve", lambda e: e.tensor_reduce(ss16, sq.rearrange("p (h d) -> p h d", d=64), AX.X, ALU.add),
                        reads=[("sq", 0), ("sq", 1)], writes=["ss16"])
                b.act(ln16, ss16, AF.Ln, r=["ss16"], w=["ln16"], bias=b.c_eps, scale=1.0 / 64.0)
                b.act(rs16, ln16, AF.Exp, r=["ln16"], w=["rs16"], scale=-0.5)
                for j in range(2):
                    pb, up = pbs[j]
                    rsb = rs16[:, j * 8:(j + 1) * 8].unsqueeze(2).to_broadcast([128, 8, 64])
                    b.tt("dve", qtmp[:, j * 512:(j + 1) * 512].rearrange("p (h d) -> p h d", d=64),
                         pb.rearrange("p (h d) -> p h d", d=64), rsb, ALU.mult, r=[up, "rs16"], w=[("qtmp", j)])
                    b.tt("pool", qn[:, j * 512:(j + 1) * 512], qtmp[:, j * 512:(j + 1) * 512], gvec[:, j * 512:(j + 1) * 512],
                         ALU.mult, r=[("qtmp", j), ugv], w=[("qn", j)])
                pt, upt = b.bank()
                ptv = pt.bitcast(BF16).rearrange("p (c n) -> p c n", c=8)
                for c in range(8):
                    b.tp(ptv[:, c, :], qn[:, c * 128:(c + 1) * 128], r=[("qn", c // 4)], w=[upt])
                b.cp("act" if which else "dve", stg[:, :, k * 128:(k + 1) * 128], ptv, r=[upt], w=[ustg])
            for j in range(2):
                pb, up = b.bank()
                for c in range(8):
                    b.mm(pb, hT[:, c, :], win[:, c, 2048 + j * 512:2048 + (j + 1) * 512], c == 0, c == 7, r=["R2", "R2b", "hT"], w=[up])
                b.cp("act", vst[:, j * 8:(j + 1) * 8, k, :], pb.rearrange("p (h d) -> p h d", d=64), r=[up], w=["vst"])
            for j in range(2 if own else 0):
                pb, up = b.bank()
                for c in range(8):
                    b.mm(pb, hT[:, c, :], win[:, c, 3088 + j * 512:3088 + (j + 1) * 512], c == 0, c == 7, r=["R2", "R2b", "hT"], w=[up])
                b.act(gsb[:, j * 512:(j + 1) * 512], pb, AF.Sigmoid, r=[up], w=[("gsb", j)])
            if own:
                pt, upt = b.bank()
                ptv = pt.bitcast(BF16).rearrange("p (c n) -> p c n", c=8)
                for c in range(8):
                    b.tp(ptv[:, c, :], gsb[:, c * 128:(c + 1) * 128], r=[("gsb", c // 4)], w=[upt])
                b.cp("dve", gst[:, :, k * 128:(k + 1) * 128], ptv, r=[upt], w=["gst"])
            pf, upf = b.bank()
            for c in range(8):
                b.mm(pf[:, 0:16], hT[:, c, :], win[:, c, 3072:3088], c == 0, c == 7, r=["R2", "R2b", "hT"], w=[upf])
            b.tt("dve", ft, pf[:, 0:16], bfb, ALU.add, r=[upf, "bfb"], w=["ft"])
            b.act(fe, ft, AF.Exp, r=["ft"], w=["fe"], scale=-1.0)
            b.act(lsp, fe, AF.Ln, r=["fe"], w=["lsp"], bias=b.c_one, scale=1.0)
            pc, upc = b.bank()
            cprev = zer if n == 0 else cs[:, n - 1, :]
            b.mm(pc[:, 0:16], b.tum1, lsp, True, False, r=["lsp"], w=[upc])
            b.mm(pc[:, 0:16], b.emat, cprev, False, True, r=[("cs", n - 1), "zer"], w=[upc])
            b.cp("dve", cs[:, n, :], pc[:, 0:16], r=[upc], w=[("cs", n)])
            b.cp("dve", c3[:, 0:16], cs[:, n, :], r=[("cs", n)], w=["c3a"])
            b.tt("dve", r1, cs[:, n, :], c3[:, 0:16], ALU.subtract, r=[("cs", n), "c3a"], w=["r1"])
            b.cp("dve", c3[:, 16:32], r1, r=["r1"], w=["c3b"])
            b.tt("dve", r2, r1, c3[:, 16:32], ALU.subtract, r=["r1", "c3b"], w=["r2"])
            b.cp("dve", c3[:, 32:48], r2, r=["r2"], w=["c3c"])
            pt, upt = b.bank()
            ptb = pt.bitcast(BF16)
            b.tp(ptb[0:48, 0:128], c3, r=["c3a", "c3b", "c3c"], w=[upt])
            b.cp("dve", cst3[0:48, k * 128:(k + 1) * 128], ptb[0:48, 0:128], r=[upt], w=["cst3"])
        t0 = ti * 512
        for c in range(8):
            pass
        b.dma("sp", scr["KT"][:, t0:t0 + 512].rearrange("(c p) t -> p c t", p=128), kst, "kst", r=["kst"])
        b.dma("sp", scr["V"][:, :, ti * 4:ti * 4 + 4, :].rearrange("h p k d -> p h k d"), vst, "vst", r=["vst"])
        b.dma("sp", scr["CQ"][:, t0:t0 + 512], cst3[0:48, :], "cst3", r=["cst3"])
        if ti >= QS:
            b.dma("sp", scr["QT"][:, t0:t0 + 512].rearrange("(c p) t -> p c t", p=128), qst, "qst", r=["qst"])
            b.dma("sp", scr["GT"][:, t0:t0 + 512].rearrange("(c p) t -> p c t", p=128), gst, "gst", r=["gst"])
    tq = b.af(4)
    b.tt("dve", tq[:, 0:1], b.Mcol, b.c_m70, ALU.mult, r=["Mcol"], w=["tq0"])
    b.stt(tq[:, 1:2], tq[:, 0:1], -1.0, b.c_m1rows, ALU.mult, ALU.add, r=["tq0"], w=["tq1"])
    if NPB:
        pbc = b.af(1)
        b.dma("sp", pbc, pbd, "pbc", w=["pbc"])
        b.tt("dve", tq[:, 2:3], pbc[:, 0:1], b.c_m71, ALU.mult, r=["pbc"], w=["tq2"])
        b.tt("dve", b.augq, tq[:, 1:2], tq[:, 2:3], ALU.add, r=["tq1", "tq2"], w=["augq"])
    else:
        b.cp("dve", b.augq, tq[:, 1:2], r=["tq1"], w=["augq"])
    b.S.barrier()


def phase_fox_attn(b, scr, wod=None, w1d_prefetch=None, QS=0):
    T = b.T
    NT = T // 512
    NB = T // 128
    b.areset()
    if wod is not None:
        wo = b.R2[:, 0:8 * 1024].rearrange("p (c n) -> p c n", c=8)
        b.load_w(wo, wod, 8, 1024, "R2", "R2")
    if w1d_prefetch is not None:
        b.load_w(b.R1.rearrange("p (c n) -> p c n", c=8), w1d_prefetch, 8, 4096, "R1", "R1")
    qt = b.ab(T)
    kt = b.ab(T)
    va = b.ab(NB * 128).rearrange("p (n d) -> p n d", d=128)
    pts = [b.ab(512) for _ in range(3)]
    rden = b.af(512)
    ots = [b.ab(512) for _ in range(2)]
    NPBk = QS * 4 * 128
    b.memset("dve", kt[64:72, :], 1.0, w=["kt1"])
    b.ts("dve", kt[64:72, NPBk:T], kt[64:72, NPBk:T], b.c_kown[64:72, 0:1], None, ALU.mult, ALU.bypass, r=["kt1"], w=["kt1"])
    b.memset("dve", qt[64:72, :], 1.0, w=["qt1"])
    b.ts("dve", qt[64:72, :], qt[64:72, :], b.augq[64:72, 0:1], None, ALU.mult, ALU.bypass, r=["qt1"], w=["qt1"])
    b.memset("pool", va[:, :, 64:128], 1.0, w=["va1"])
    cqv = scr["CQ"].rearrange("(j h) t -> h j t", h=16)

    def load_chunk(h, c):
        cs_ = slice(c * 512, (c + 1) * 512)
        if c >= QS:
            b.dma("sp", qt[0:64, cs_], scr["QT"][h * 64:(h + 1) * 64, cs_], "aq%d" % c, w=[("qt", c)])
            b.dma("sp", qt[64:67, cs_], cqv[h][:, cs_], "aq%d" % c, r=["qt1"], w=[("qt", c)])
        b.dma("sp", kt[0:64, cs_], scr["KT"][h * 64:(h + 1) * 64, cs_], "ak%d" % c, w=[("kt", c)])
        b.dma("sp", kt[67:70, cs_], cqv[h][:, cs_], "ak%d" % c, r=["kt1"], w=[("kt", c)])
        b.dma("sp", va[:, c * 4:(c + 1) * 4, 0:64], scr["V"][h][:, c * 4:(c + 1) * 4, :], "av%d" % c, w=[("va", c)])

    blocks = []
    for h in range(16):
        for qi in reversed(range(QS, NT)):
            nkb = 4 * (qi + 1)
            for kb in range(nkb):
                blocks.append((h, qi, kb, nkb))

    def geom(blk):
        h, qi, kb, nkb = blk
        j = kb - 4 * qi
        q0 = 128 * j if j > 0 else 0
        return h, qi, kb, nkb, j, q0

    def emit_st(idx):
        h, qi, kb, nkb, j, q0 = geom(blocks[idx])
        sti = 2 + (idx % 4)
        ps_, ups = b.banks[sti], ("ps", sti)
        b.mm(ps_[:, q0:512], kt[0:72, kb * 128:(kb + 1) * 128], qt[0:72, qi * 512 + q0:(qi + 1) * 512], True, j < 0,
             r=[("kt", kb // 4), "kt1", ("qt", qi), "qt1"], w=[ups])
        if j >= 0:
            b.mm(ps_[:, q0:q0 + 128], b.ident, b.mneg, False, True, r=[], w=[ups])

    def emit_rest(idx):
        h, qi, kb, nkb, j, q0 = geom(blocks[idx])
        sti = 2 + (idx % 4)
        ps_, ups = b.banks[sti], ("ps", sti)
        po, upo = b.banks[qi % 2], ("ps", qi % 2)
        pt = pts[idx % 3]
        upt = ("pt", idx % 3)
        b.act(pt[:, q0:512], ps_[:, q0:512], AF.Exp, r=[ups], w=[upt])
        b.mm(po[:, q0:512], va[:, kb, :], pt[:, q0:512], kb == 0, kb == nkb - 1, r=[("va", kb // 4), "va1", upt], w=[upo])
        if kb == nkb - 1:
            b.S.add("dve", lambda e: e.reciprocal(rden[0:64, :], po[64:128, :]), reads=[upo], writes=["rden"], cost=4.4)
            ot = ots[qi % 2]
            b.tt("dve", ot[0:64, :], po[0:64, :], rden[0:64, :], ALU.mult, r=[upo, "rden"], w=[("ot", qi % 2)])
            b.dma("sp", scr["OT"][h * 64:(h + 1) * 64, qi * 512:(qi + 1) * 512], ot[0:64, :], "ot%d" % (qi % 2), r=[("ot", qi % 2)])
            if h + 1 < 16:
                load_chunk(h + 1, qi)
                if qi == QS:
                    for c in reversed(range(QS)):
                        load_chunk(h + 1, c)

    for c in reversed(range(NT)):
        load_chunk(0, c)
    LA = 2
    st_ptr = 0
    nblk = len(blocks)
    for idx in range(nblk):
        while st_ptr < nblk and st_ptr <= idx + LA and (st_ptr <= idx or blocks[st_ptr][0] == blocks[idx][0]):
            emit_st(st_ptr)
            st_ptr += 1
        emit_rest(idx)
    b.S.barrier()


def phase_fox_out(b, xin, xout, scr, wod=None, QS=0):
    T = b.T
    NT = T // 512
    b.areset()
    wo = b.R2[:, 0:8 * 1024].rearrange("p (c n) -> p c n", c=8)
    if wod is not None:
        b.load_w(wo, wod, 8, 1024, "R2", "R2")
    xt = b.af(4 * 1024).rearrange("p (k n) -> p k n", k=4)
    ot = b.ab(8 * 512).rearrange("p (c n) -> p c n", c=8)
    gt = b.ab(8 * 512).rearrange("p (c n) -> p c n", c=8)
    og = b.ab(8 * 512).rearrange("p (c n) -> p c n", c=8)
    for ti in range(QS, NT):
        t0 = ti * 512
        for k in range(4):
            r0 = (ti * 4 + k) * 128
            b.dma("sp", xt[:, k, :], xin[r0:r0 + 128, :], "xt%d" % k, w=[("xt", k)])
        b.dma("sp", ot, scr["OT"][:, t0:t0 + 512].rearrange("(c p) t -> p c t", p=128), "ot", w=["ot"])
        b.dma("sp", gt, scr["GT"][:, t0:t0 + 512].rearrange("(c p) t -> p c t", p=128), "gt", w=["gt"])
        for c in range(8):
            b.tt("pool" if c % 2 else "dve", og[:, c, :], ot[:, c, :], gt[:, c, :], ALU.mult, r=["ot", "gt"], w=[("og", c)])
        ogu = [("og", c) for c in range(8)]
        for k in range(4):
            for n2 in range(2):
                py, upy = b.bank()
                for c in range(8):
                    b.mm(py, og[:, c, k * 128:(k + 1) * 128], wo[:, c, n2 * 512:(n2 + 1) * 512], c == 0, c == 7, r=["R2", "R2b"] + ogu, w=[upy])
                xs = xt[:, k, n2 * 512:(n2 + 1) * 512]
                b.tt("dve", xs, xs, py, ALU.add, r=[upy, ("xt", k)], w=[("xt", k)])
            r0 = (ti * 4 + k) * 128
            b.dma("sp", xout[r0:r0 + 128, :], xt[:, k, :], "xt%d" % k, r=[("xt", k)])
    b.S.barrier()


WNAMES = [
    ("l0_w_in", [1024, 3088]), ("l0_w_out", [1024, 1024]), ("l0_w_ff1", [1024, 4096]), ("l0_w_ff2", [4096, 1024]),
    ("l1_w_in", [1024, 4112]), ("l1_w_out", [1024, 1024]), ("l1_w_ff1", [1024, 4096]), ("l1_w_ff2", [4096, 1024]),
]
VNAMES = [
    ("l0_norm_mix", 1024), ("l0_norm_ffn", 1024), ("l1_norm_mix", 1024), ("l1_norm_ffn", 1024), ("final_norm", 1024),
    ("l0_g_onorm", 1024), ("l1_g_q", 1024), ("l1_g_k", 1024), ("l1_b_f", 16),
]


def build_program(T, phases=("gla", "mlp0", "fox", "mlp1"), QS=0):
    nc = bass.Bass("TRN2", target_bir_lowering=False)
    dr = {}
    dr["x"] = nc.dram_tensor("x", [T, D], F32, kind="ExternalInput").ap()
    for nme, shp in WNAMES:
        dr[nme] = nc.dram_tensor(nme, shp, F32, kind="ExternalInput").ap()
    for nme, n in VNAMES:
        dr[nme] = nc.dram_tensor(nme, [128, n], F32, kind="ExternalInput").ap()
    dr["wgk"] = nc.dram_tensor("wgk", [17, 512], F32, kind="ExternalInput").ap()
    dr["cst"] = nc.dram_tensor("cst", [128, NCST], F32, kind="ExternalInput").ap()
    TO = T - QS * 512
    out_tiles = nc.dram_tensor("out", [TO, D], F32, kind="ExternalOutput").ap()
    dr["pbias"] = nc.dram_tensor("pbias", [128, 1], F32, kind="ExternalInput").ap()
    xr = nc.dram_tensor("xr", [T, D], F32, kind="Internal").ap()
    scr = {
        "QT": nc.dram_tensor("sQT", [1024, T], BF16, kind="Internal").ap(),
        "KT": nc.dram_tensor("sKT", [1024, T], BF16, kind="Internal").ap(),
        "GT": nc.dram_tensor("sGT", [1024, T], BF16, kind="Internal").ap(),
        "OT": nc.dram_tensor("sOT", [1024, T], BF16, kind="Internal").ap(),
        "V": nc.dram_tensor("sV", [16, 128, T // 128, 64], BF16, kind="Internal").ap(),
        "CQ": nc.dram_tensor("sCQ", [48, T], BF16, kind="Internal").ap(),
    }
    with ExitStack() as st:
        b = Bld(nc, st, T)
        b.R1 = b.sb("R1", [128, 8 * 4096], BF16)
        b.R2 = b.sb("R2", [128, 8 * 4112], BF16)
        b.ar = b.sb("arena", [128, AW], F32)
        cstf = b.sb("cstf", [128, NCST], F32)
        cstb = b.sb("cstb", [128, 256], BF16)
        b.gv = b.sb("gv", [128, 1024], F32)
        NB = T // 128
        b.cs = b.sb("cs", [128, NB, 16], F32)
        b.Mcol = b.sb("Mcol", [128, 1], F32)
        b.banks = [st.enter_context(nc.psum_tensor("bank%d" % i, [128, 512], F32))[:] for i in range(8)]
        b.gv = b.gv[:]
        b.cs = b.cs[:]
        b.Mcol = b.Mcol[:]
        b.R1 = b.R1[:]
        b.R2 = b.R2[:]
        b.dma("sp", cstf[:], dr["cst"], "cstf", w=["cstf"])
        b.dma("pool", cstb[:, 0:128], dr["cst"][:, C_ID:C_ID + 128], "cstb", w=["cstb"])
        b.dma("pool", cstb[:, 128:256], dr["cst"][:, C_MNEG:C_MNEG + 128], "cstb", w=["cstb"])
        b.S.barrier()
        b.ident = cstb[:, 0:128]
        b.mask01 = cstf[:, C_M01:C_M01 + 128]
        b.mneg = cstb[:, 128:256]
        b.tu16 = cstf[:, C_TU16:C_TU16 + 128]
        b.tl16 = cstf[:, C_TL16:C_TL16 + 128]
        b.tum1 = cstf[:, C_TUM1:C_TUM1 + 128]
        b.emat = cstf[:, C_E:C_E + 128]
        b.c_eps = cstf[:, C_COL:C_COL + 1]
        b.c_one = cstf[:, C_COL + 1:C_COL + 2]
        b.m16col2 = cstf[:, C_COL + 2:C_COL + 4]
        b.c_m1rows = cstf[:, C_COL + 4:C_COL + 5]
        b.c_m70 = cstf[:, C_COL + 5:C_COL + 6]
        b.c_m71 = cstf[:, C_COL + 6:C_COL + 7]
        b.c_kown = cstf[:, C_COL + 7:C_COL + 8]
        b.augq = b.sb("augq", [128, 1], F32)[:]
        cur = dr["x"]
        last = phases[-1]

        def dst(ph):
            return out_tiles if ph == last else xr

        for ph in phases:
            if ph == "gla":
                phase_gla(b, cur, dst(ph), dr["l0_w_in"], dr["wgk"], dr["l0_w_out"], dr["l0_norm_mix"], dr["l0_g_onorm"],
                          w1d_prefetch=None)
                cur = xr
            elif ph == "mlp0":
                phase_mlp(b, cur, dst(ph), dr["l0_w_ff1"], dr["l0_w_ff2"], dr["l0_norm_ffn"], w1_loaded=False)
                cur = xr
            elif ph == "fox":
                phase_fox_proj(b, cur, dr["l1_w_in"], dr["l1_norm_mix"], dr["l1_g_q"], dr["l1_g_k"], dr["l1_b_f"], scr,
                               NPB=QS * 4, pbd=dr["pbias"], QS=QS)
                phase_fox_attn(b, scr, wod=dr["l1_w_out"], w1d_prefetch=dr["l1_w_ff1"] if "mlp1" in phases else None, QS=QS)
                if ph == last:
                    assert QS == 0
                phase_fox_out(b, cur, dst(ph), scr, QS=QS)
                cur = xr
            elif ph == "mlp1":
                phase_mlp(b, cur, dst(ph), dr["l1_w_ff1"], dr["l1_w_ff2"], dr["l1_norm_ffn"], final_gd=dr["final_norm"],
                          w1_loaded=("fox" in phases), QS=QS, out_off=QS * 512)
                cur = xr
        b.S.barrier()
        b.S.run(st, reorder=(os.environ.get("NOREORDER") is None))
    return nc


def host_inputs(inputs):
    f = lambda a: np.ascontiguousarray(np.asarray(a, dtype=np.float32))
    m = {}
    for nme, _ in WNAMES:
        m[nme] = f(inputs[nme])
    bc = lambda v, reps=1: f(np.broadcast_to(np.tile(np.asarray(v, np.float32), reps)[None, :], (128, np.asarray(v).size * reps)))
    for nme in ("l0_norm_mix", "l0_norm_ffn", "l1_norm_mix", "l1_norm_ffn", "final_norm"):
        m[nme] = bc(inputs[nme])
    m["l0_g_onorm"] = bc(inputs["l0_g_onorm"], 4)
    m["l1_g_q"] = bc(inputs["l1_g_q"], 16)
    m["l1_g_k"] = bc(inputs["l1_g_k"], 16)
    m["l1_b_f"] = bc(inputs["l1_b_f"])
    m["wgk"] = f(np.concatenate([np.asarray(inputs["l0_w_gk_up"], np.float32), np.asarray(inputs["l0_b_gk"], np.float32)[None, :]], axis=0))
    m["cst"] = make_consts()
    return m


_PROG_CACHE = {}


def kernel(**inputs):
    x = np.asarray(inputs["x"], dtype=np.float32)
    B, S, _ = x.shape
    T = S
    H = S // 2
    QS = (S // 512) // 2
    key = (T, QS)
    if key not in _PROG_CACHE:
        _PROG_CACHE[key] = build_program(T, QS=QS)
    nc = _PROG_CACHE[key]
    shared = host_inputs(inputs)
    in_maps = []
    for bi in range(B):
        for r in range(2):
            m = dict(shared)
            if r == 0:
                m["x"] = np.ascontiguousarray(np.concatenate([np.zeros((H, D), np.float32), x[bi, :H]], axis=0))
                m["pbias"] = np.full((128, 1), -30000.0, np.float32)
            else:
                m["x"] = np.ascontiguousarray(x[bi])
                m["pbias"] = np.zeros((128, 1), np.float32)
            in_maps.append(m)
    res = run_bass_kernel_spmd(nc, in_maps, core_ids=list(range(2 * B)))
    out = np.empty((B, S, D), np.float32)
    for bi in range(B):
        for r in range(2):
            out[bi, r * H:(r + 1) * H] = np.asarray(res.results[2 * bi + r]["out"], dtype=np.float32)
    return out
```

```python
import os
import numpy as np
from contextlib import ExitStack
import concourse.bass as bass
import concourse.mybir as mybir
from concourse.bass_utils import run_bass_kernel_spmd

F32 = mybir.dt.float32
BF16 = mybir.dt.bfloat16
AF = mybir.ActivationFunctionType
ALU = mybir.AluOpType
AX = mybir.AxisListType
ENGS = ["pe", "act", "dve", "pool", "sp"]
D = 1024
EPS = 1e-6


class Op:
    __slots__ = ("eng", "fn", "deps", "alldeps", "signal", "dma_key", "dma_cnt", "sigcount", "cost", "idx", "bytes",
                 "succ", "nin", "rt", "fin")


LAT_X = 0.60
LAT_S = 0.15


class Sched:
    def __init__(self, nc):
        self.nc = nc
        self.ops = {e: [] for e in ENGS}
        self.last_w = {}
        self.readers = {}
        self.dma_cnt = {}
        self.dma_last = {}
        self.sems = {}
        self.n = 0

    def add(self, eng, fn, reads=(), writes=(), dma_key=None, cost=0.2, nbytes=0):
        op = Op()
        op.eng = eng
        op.fn = fn
        op.signal = False
        op.dma_key = dma_key
        op.dma_cnt = 0
        op.sigcount = 0
        op.cost = cost
        op.bytes = nbytes
        op.idx = self.n
        self.n += 1
        deps = []
        order_only = []
        for u in reads:
            w = self.last_w.get(u)
            if w is not None:
                deps.append(w)
        for u in writes:
            w = self.last_w.get(u)
            if w is not None:
                if dma_key is not None and w.dma_key == dma_key and eng != "pool":
                    deps.extend(w.deps)
                    order_only.append(w)
                else:
                    deps.append(w)
            deps.extend(self.readers.get(u, ()))
        dd = []
        ad = []
        seen = set()
        for d in deps:
            if id(d) in seen or d is op:
                continue
            seen.add(id(d))
            ad.append(d)
            if d.dma_key is None and d.eng == "pe" and eng == "pe" and dma_key is None:
                continue
            dd.append(d)
            if d.dma_key is None:
                d.signal = True
        for d in order_only:
            if id(d) not in seen:
                seen.add(id(d))
                ad.append(d)
        op.deps = dd
        op.alldeps = ad
        for u in reads:
            self.readers.setdefault(u, []).append(op)
        for u in writes:
            self.last_w[u] = op
            self.readers[u] = []
        if dma_key is not None:
            self.dma_cnt[dma_key] = self.dma_cnt.get(dma_key, 0) + 1
            op.dma_cnt = self.dma_cnt[dma_key]
            self.dma_last[dma_key] = op
        self.ops[eng].append(op)
        return op

    def barrier(self):
        dl = list(self.dma_last.values())
        for e in ENGS:
            op = Op()
            op.eng = e
            op.fn = None
            op.signal = False
            op.dma_key = None
            op.dma_cnt = 0
            op.sigcount = 0
            op.cost = 0.0
            op.bytes = 0
            op.idx = self.n
            op.deps = dl
            op.alldeps = []
            self.ops[e].append(op)
        self.n += 1
        self.last_w = {}
        self.readers = {}

    def _fill_barriers(self):
        pos = {e: 0 for e in ENGS}
        while True:
            lasts = []
            bars = []
            done = False
            for e in ENGS:
                lst = self.ops[e]
                j = pos[e]
                last = None
                while j < len(lst) and lst[j].fn is not None:
                    if lst[j].dma_key is None:
                        last = lst[j]
                    j += 1
                if j >= len(lst):
                    done = True
                    break
                bars.append(lst[j])
                k = j - 1
                while last is None and k >= 0:
                    if lst[k].fn is not None and lst[k].dma_key is None:
                        last = lst[k]
                    k -= 1
                if last is not None:
                    lasts.append(last)
                pos[e] = j + 1
            if done:
                break
            for l_ in lasts:
                l_.signal = True
            for bop in bars:
                bop.deps = list(bop.deps) + lasts

    def check(self):
        cnt = {}
        pos = {e: 0 for e in ENGS}
        total = sum(len(v) for v in self.ops.values())
        donec = 0
        progress = True
        while progress:
            progress = False
            for e in ENGS:
                lst = self.ops[e]
                while pos[e] < len(lst):
                    op = lst[pos[e]]
                    ok = True
                    for d in op.deps:
                        if d.dma_key is not None:
                            if cnt.get(("dma", d.dma_key), 0) < d.dma_cnt * 16:
                                ok = False
                                break
                        else:
                            if cnt.get(("eng", d.eng), 0) < d.sigcount:
                                ok = False
                                break
                    if not ok:
                        break
                    if op.fn is not None:
                        if op.dma_key is not None:
                            k = ("dma", op.dma_key)
                            cnt[k] = cnt.get(k, 0) + 16
                        elif op.signal:
                            k = ("eng", op.eng)
                            cnt[k] = cnt.get(k, 0) + 1
                    pos[e] += 1
                    donec += 1
                    progress = True
        if donec != total:
            msg = []
            for e in ENGS:
                if pos[e] < len(self.ops[e]):
                    op = self.ops[e][pos[e]]
                    msg.append((e, pos[e], len(self.ops[e]), op.idx, [(d.eng, d.dma_key, d.dma_cnt, d.sigcount, d.idx) for d in op.deps][:6]))
            raise RuntimeError("DEADLOCK in emitted program: %s" % (msg,))

    def _schedule_segment(self, seg):
        import heapq
        allops = []
        for e in ENGS:
            allops.extend(seg[e])
        inseg = set(id(o) for o in allops)
        for o in allops:
            o.succ = []
            o.nin = 0
            o.rt = 0.0
            o.fin = 0.0
        for o in allops:
            for d in o.alldeps:
                if id(d) in inseg:
                    d.succ.append(o)
                    o.nin += 1
        avail = {e: [] for e in ENGS}
        future = {e: [] for e in ENGS}
        free = {e: 0.0 for e in ENGS}
        dma_free = [0.0]
        out = {e: [] for e in ENGS}
        for o in allops:
            if o.nin == 0:
                heapq.heappush(future[o.eng], (0.0, o.idx, o))
        remaining = len(allops)
        while remaining:
            best = None
            for e in ENGS:
                fu = future[e]
                av = avail[e]
                while fu and fu[0][0] <= free[e]:
                    r, i, o = heapq.heappop(fu)
                    heapq.heappush(av, (i, o))
                if av:
                    st = free[e]
                elif fu:
                    st = fu[0][0]
                else:
                    continue
                if best is None or st < best[0]:
                    best = (st, e)
            st, e = best
            if avail[e]:
                i, o = heapq.heappop(avail[e])
            else:
                r, i, o = heapq.heappop(future[e])
            if o.dma_key is not None:
                xs = max(st, dma_free[0])
                dur = o.bytes / 1.6e5
                dma_free[0] = xs + dur
                o.fin = xs + dur + 2.0
                free[e] = st + (0.6 if e == "pool" else 0.12)
            else:
                o.fin = st + o.cost
                free[e] = o.fin
            out[e].append(o)
            remaining -= 1
            for s_ in o.succ:
                lat = 0.0 if (s_.eng == "pe" and o.eng == "pe" and o.dma_key is None) else (LAT_S if s_.eng == o.eng and o.dma_key is None else LAT_X)
                t = o.fin + lat
                if t > s_.rt:
                    s_.rt = t
                s_.nin -= 1
                if s_.nin == 0:
                    heapq.heappush(future[s_.eng], (s_.rt, s_.idx, s_))
        return out, max([0.0] + [o.fin for o in allops])

    def schedule(self):
        nseg = sum(1 for o in self.ops["pe"] if o.fn is None)
        cur = {e: 0 for e in ENGS}
        newops = {e: [] for e in ENGS}
        total = 0.0
        while True:
            seg = {}
            bar = {}
            for e in ENGS:
                lst = self.ops[e]
                i = cur[e]
                j = i
                while j < len(lst) and lst[j].fn is not None:
                    j += 1
                seg[e] = lst[i:j]
                bar[e] = lst[j] if j < len(lst) else None
                cur[e] = j + 1
            if any(seg[e] for e in ENGS):
                out, t = self._schedule_segment(seg)
                total += t
                self.seg_times = getattr(self, "seg_times", []) + [t]
            else:
                out = seg
            for e in ENGS:
                newops[e].extend(out[e])
                if bar[e] is not None:
                    newops[e].append(bar[e])
            if all(cur[e] >= len(self.ops[e]) for e in ENGS):
                break
        self.ops = newops
        self.est_total = total

    def run(self, stack, reorder=True):
        if reorder:
            self.schedule()
        self._fill_barriers()
        for e in ENGS:
            c = 0
            for o in self.ops[e]:
                if o.dma_key is None and o.signal and o.fn is not None:
                    c += 1
                o.sigcount = c
        self.check()
        keys = [("eng", e) for e in ENGS] + [("dma", k) for k in self.dma_cnt]
        for k in keys:
            self.sems[k] = stack.enter_context(self.nc.semaphore("s_" + "_".join(str(x) for x in k)))
        blk = stack.enter_context(self.nc.Block())

        @blk.tensor
        def _(e):
            self.emit("pe", e)

        @blk.scalar
        def _(e):
            self.emit("act", e)

        @blk.vector
        def _(e):
            self.emit("dve", e)

        @blk.gpsimd
        def _(e):
            self.emit("pool", e)

        @blk.sync
        def _(e):
            self.emit("sp", e)

    def emit(self, eng_name, eng):
        waited = {}
        for op in self.ops[eng_name]:
            need = {}
            for d in op.deps:
                if d.dma_key is not None:
                    k = ("dma", d.dma_key)
                    v = d.dma_cnt * 16
                else:
                    k = ("eng", d.eng)
                    v = d.sigcount
                if need.get(k, 0) < v:
                    need[k] = v
            for k, v in need.items():
                if waited.get(k, 0) < v:
                    eng.wait_ge(self.sems[k], v)
                    waited[k] = v
            if op.fn is None:
                continue
            inst = op.fn(eng)
            if op.dma_key is not None:
                inst.then_inc(self.sems[("dma", op.dma_key)], 16)
            elif op.signal:
                inst.then_inc(self.sems[("eng", op.eng)], 1)


C_ID, C_TU16, C_TL16, C_TUM1, C_E, C_M01, C_MNEG, C_COL = 0, 128, 256, 384, 512, 640, 768, 896
NCST = 908


def make_consts():
    c = np.zeros((128, NCST), np.float32)
    s = np.arange(128)[:, None]
    t = np.arange(128)[None, :]
    c[:, C_ID:C_ID + 128] = (s == t)
    c[:, C_TU16:C_TU16 + 128] = np.where(s <= t, -1.0 / 16.0, 0.0)
    c[:, C_TL16:C_TL16 + 128] = np.where(s > t, -1.0 / 16.0, 0.0)
    c[:, C_TUM1:C_TUM1 + 128] = np.where(s <= t, -1.0, 0.0)
    c[:, C_E:C_E + 128] = (s == 127)
    c[:, C_M01:C_M01 + 128] = (t >= s)
    c[:, C_MNEG:C_MNEG + 128] = np.where(t >= s, 0.0, -30000.0)
    c[:, C_COL + 0] = EPS
    c[:, C_COL + 1] = 1.0
    c[:, C_COL + 2] = -1.0 / 16.0
    c[:, C_COL + 3] = 0.0
    p = np.arange(128)
    c[:, C_COL + 4] = np.where((p >= 67) & (p <= 69), -1.0, 0.0)
    c[:, C_COL + 5] = (p == 70)
    c[:, C_COL + 6] = (p == 71)
    c[:, C_COL + 7] = (p != 71)
    return c


def fsz(ap):
    return int(np.prod(ap.shape[1:]))


AW = 16912


class Bld:
    def __init__(self, nc, st, T):
        self.nc = nc
        self.st = st
        self.T = T
        self.S = Sched(nc)
        self.apos = 0
        self.bank_i = 0
        self.bank_i1 = 0
        self.bank_i2 = 0

    def sb(self, name, shape, dt):
        return self.st.enter_context(self.nc.sbuf_tensor(name, shape, dt))

    def areset(self):
        self.apos = 0

    def af(self, n):
        off = self.apos
        self.apos += n
        assert self.apos <= AW, ("arena overflow", self.apos)
        return self.ar[:, off:off + n]

    def ab(self, n):
        return self.af((n + 1) // 2).bitcast(BF16)

    def af1(self, n):
        off = self.r1pos
        self.r1pos += n
        assert self.r1pos <= 16384
        return self.R1.bitcast(F32)[:, off:off + n]

    def ab1(self, n):
        return self.af1((n + 1) // 2).bitcast(BF16)

    def bank(self, pool=0):
        if pool == 0:
            i = self.bank_i % 8
            self.bank_i += 1
        elif pool == 1:
            i = self.bank_i1 % 4
            self.bank_i1 += 1
        else:
            i = 4 + self.bank_i2 % 4
            self.bank_i2 += 1
        return self.banks[i], ("ps", i)

    def mm(self, out, lhsT, rhs, start, stop, r, w):
        n = fsz(rhs)
        c = (4.0 * max(n, 96) / 1950.0 + 0.05) if lhsT.dtype == F32 else (max(n, 96) / 1950.0 + 0.005)
        self.S.add("pe", lambda e: e.matmul(out, lhsT=lhsT, rhs=rhs, start=start, stop=stop), reads=r, writes=w, cost=c)

    def tp(self, out, in_, r, w):
        ident = self.ident
        self.S.add("pe", lambda e: e.transpose(out, in_, ident), reads=r, writes=w, cost=0.075)

    def act(self, out, in_, func, r, w, bias=None, scale=None, accum=None):
        kw = {}
        if bias is not None:
            kw["bias"] = bias
        if scale is not None:
            kw["scale"] = scale
        if accum is not None:
            kw["accum_out"] = accum
        c = 0.13 + fsz(in_) / 1400.0 + (0.1 if accum is not None else 0.0) + (0.09 if bias is not None and not isinstance(bias, float) else 0.0)
        self.S.add("act", lambda e: e.activation(out, in_, func, **kw), reads=r, writes=w, cost=c)

    def tt(self, eng, out, in0, in1, op, r, w):
        self.S.add(eng, lambda e: e.tensor_tensor(out, in0, in1, op), reads=r, writes=w, cost=0.12 + fsz(in0) / 960.0)

    def stt(self, out, in0, scalar, in1, op0, op1, r, w):
        self.S.add("dve", lambda e: e.scalar_tensor_tensor(out, in0, scalar, in1, op0, op1), reads=r, writes=w,
                   cost=0.12 + fsz(in0) / 960.0)

    def ts(self, eng, out, in0, s1, s2, op0, op1, r, w):
        self.S.add(eng, lambda e: e.tensor_scalar(out, in0, s1, s2, op0, op1), reads=r, writes=w, cost=0.12 + fsz(in0) / 960.0)

    def cp(self, eng, out, in_, r, w):
        if eng == "act":
            self.S.add("act", lambda e: e.activation(out, in_, AF.Copy), reads=r, writes=w, cost=0.22 + fsz(in_) / 1400.0)
        else:
            self.S.add(eng, lambda e: e.tensor_copy(out, in_), reads=r, writes=w, cost=0.12 + fsz(in_) / 960.0)

    def memset(self, eng, ap, val, w):
        self.S.add(eng, lambda e: e.memset(ap, val), writes=w, cost=0.12 + fsz(ap) / 960.0)

    def dma(self, q, out, in_, key, r=(), w=()):
        nb = int(np.prod(out.shape)) * (2 if out.dtype == BF16 else 4)
        self.S.add(q, lambda e: e.dma_start(out=out, in_=in_), reads=r, writes=w, dma_key=key, nbytes=nb)

    def load_w(self, dst3, src2d, C, N, key, unit, rearr=False):
        i = 0
        for c in range(C):
            for a in range(0, N, 2048):
                b_ = min(N, a + 2048)
                sfx = "b" if i % 2 else ""
                self.dma("pool", dst3[:, c, a:b_], src2d[c * 128:(c + 1) * 128, a:b_], key + sfx, w=[unit + sfx])
                i += 1


def norm_block(b, xt, ux, gv, ugv, hb, uhb, tmp):
    sf = tmp.get("sfx", "")
    b.act(tmp["junk"], xt, AF.Square, r=[ux], w=["ss" + sf, tmp["ujunk"]], accum=tmp["ss"])
    b.act(tmp["lnv"], tmp["ss"], AF.Ln, r=["ss" + sf], w=["lnv" + sf], bias=b.c_eps, scale=1.0 / D)
    b.act(tmp["rstd"], tmp["lnv"], AF.Exp, r=["lnv" + sf], w=["rstd" + sf], scale=-0.5)
    b.stt(hb, xt, tmp["rstd"], gv, ALU.mult, ALU.mult, r=[ux, "rstd" + sf, ugv], w=[uhb])


def transpose_to(b, src, usrc, dst3, udst, cp_eng, pool=0):
    pb, up = b.bank(pool)
    pv = pb.bitcast(BF16).rearrange("p (c n) -> p c n", c=8)
    for c in range(8):
        b.tp(pv[:, c, :], src[:, c * 128:(c + 1) * 128], r=[usrc], w=[up])
    b.cp(cp_eng, dst3, pv, r=[up], w=[udst])


def load_gvec(b, dram_ap, dst, unit, key="gv"):
    b.dma("sp", dst, dram_ap, key, w=[unit])


def phase_mlp(b, xin, xout, w1d, w2d, gd, final_gd=None, w1_loaded=False, QS=0, out_off=0):
    T = b.T
    NT = T // 512
    b.areset()
    w1 = b.R1.rearrange("p (c n) -> p c n", c=8)
    w2 = b.R2[:, 0:32 * 1024].rearrange("p (f n) -> p f n", f=32)
    if not w1_loaded:
        b.load_w(w1, w1d, 8, 4096, "R1", "R1")
    for f0 in range(0, 32, 4):
        sfx = "b" if (f0 // 4) % 2 else ""
        b.dma("pool", w2[:, f0:f0 + 4, :], w2d[f0 * 128:(f0 + 4) * 128, :].rearrange("(f p) n -> p f n", p=128), "R2" + sfx, w=["R2" + sfx])
    load_gvec(b, gd, b.gv, "gv")
    gv2 = None
    if final_gd is not None:
        gv2 = b.af(1024)
        load_gvec(b, final_gd, gv2, "gv2", key="gv2")
    xt = b.af(4 * 1024).rearrange("p (k n) -> p k n", k=4)
    hb1 = b.ab(1024)
    hbs = [hb1, hb1]
    hT = b.ab(8 * 512).rearrange("p (c n) -> p c n", c=8)
    uT = b.ab(32 * 512).rearrange("p (f n) -> p f n", f=32)
    rts = [b.af(512), b.af(512)]
    sm = b.af(8)
    tmp = {"ss": sm[:, 0:1], "lnv": sm[:, 1:2], "rstd": sm[:, 2:3], "junk": rts[0].bitcast(BF16), "ujunk": ("rt", 0)}
    for ti in range(QS, NT):
        for k in range(4):
            r0 = (ti * 4 + k) * 128
            b.dma("sp", xt[:, k, :], xin[r0:r0 + 128, :], "xt%d" % k, w=[("xt", k)])
        for k in range(4):
            hb = hbs[0]
            uhb = ("hb", 0)
            norm_block(b, xt[:, k, :], ("xt", k), b.gv, "gv", hb, uhb, tmp)
            transpose_to(b, hb, uhb, hT[:, :, k * 128:(k + 1) * 128], ("hT", k), "dve" if k % 2 else "act")
        hTu = [("hT", k) for k in range(4)]
        for f in range(32):
            pb, up = b.bank()
            for c in range(8):
                b.mm(pb, w1[:, c, f * 128:(f + 1) * 128], hT[:, c, :], c == 0, c == 7, r=["R1", "R1b"] + hTu, w=[up])
            rt = rts[f % 2]
            b.act(rt, pb, AF.Relu, r=[up], w=[("rt", f % 2)])
            b.tt("pool" if f % 2 else "dve", uT[:, f, :], rt, rt, ALU.mult, r=[("rt", f % 2)], w=[("uT", f)])
        uTu = [("uT", f) for f in range(32)]
        for k in range(4):
            for n2 in range(2):
                pb, up = b.bank()
                for f in range(32):
                    b.mm(pb, uT[:, f, k * 128:(k + 1) * 128], w2[:, f, n2 * 512:(n2 + 1) * 512], f == 0, f == 31,
                         r=["R2", "R2b"] + uTu, w=[up])
                xs = xt[:, k, n2 * 512:(n2 + 1) * 512]
                b.tt("dve", xs, xs, pb, ALU.add, r=[up, ("xt", k)], w=[("xt", k)])
            r0 = (ti * 4 + k) * 128
            ro = r0 - out_off
            if final_gd is None:
                b.dma("sp", xout[ro:ro + 128, :], xt[:, k, :], "xt%d" % k, r=[("xt", k)])
            else:
                b.act(tmp["junk"], xt[:, k, :], AF.Square, r=[("xt", k)], w=["ss", tmp["ujunk"]], accum=tmp["ss"])
                b.act(tmp["lnv"], tmp["ss"], AF.Ln, r=["ss"], w=["lnv"], bias=b.c_eps, scale=1.0 / D)
                b.act(tmp["rstd"], tmp["lnv"], AF.Exp, r=["lnv"], w=["rstd"], scale=-0.5)
                b.stt(xt[:, k, :], xt[:, k, :], tmp["rstd"], gv2, ALU.mult, ALU.mult, r=[("xt", k), "rstd", "gv2"], w=[("xt", k)])
                b.dma("sp", xout[ro:ro + 128, :], xt[:, k, :], "xt%d" % k, r=[("xt", k)])
    b.S.barrier()


def phase_gla(b, xin, xout, wind, wgkd, wod, gd, gond, w1d_prefetch=None):
    T = b.T
    NB = T // 128
    b.areset()
    NIN = 3088
    win = b.R2[:, 0:8 * NIN].rearrange("p (c n) -> p c n", c=8)
    wo = b.R2[:, 8 * NIN:8 * NIN + 8 * 1024].rearrange("p (c n) -> p c n", c=8)
    b.load_w(win, wind, 8, NIN, "R2", "R2")
    b.load_w(wo, wod, 8, 1024, "R2", "R2")
    load_gvec(b, gd, b.gv, "gv")
    gv2 = b.af(1024)
    load_gvec(b, gond, gv2, "gv2", key="gv2")
    if w1d_prefetch is not None:
        b.load_w(b.R1.rearrange("p (c n) -> p c n", c=8), w1d_prefetch, 8, 4096, "R1", "R1")
    wgk = b.af(512)
    b.dma("sp", wgk[0:17, :], wgkd, "wgk", w=["wgk"])
    gkT = b.af(128)
    b.memset("dve", gkT[0:32, :], 1.0, w=[("gkT", 0)])
    Sst = b.af(1024).rearrange("p (h n) -> p h n", h=4)
    Sbf = b.ab(1024).rearrange("p (h n) -> p h n", h=4)
    b.memset("dve", Sst, 0.0, w=["S"])
    b.memset("pool", Sbf, 0.0, w=["Sbf"])
    b.r1pos = 0
    xts = [b.af(1024), b.af(1024)]
    hbs = [b.ab(1024), b.ab1(1024)]
    hTs = [b.ab(1024).rearrange("p (c n) -> p c n", c=8), b.ab1(1024).rearrange("p (c n) -> p c n", c=8)]
    ets = [b.af(512), b.af1(512)]
    lses = [b.af(512), b.af1(512)]
    ebs = [b.af(512), b.af1(512)]
    enbs = [b.af(512), b.af1(512)]
    ebrs = [b.af(512), b.af1(512)]
    decs = b.af(8)
    qds = [b.ab(512), b.ab1(512)]
    kis = [b.ab(512), b.ab1(512)]
    kds = [b.ab(512), b.ab(512)]
    qkTs = [b.ab(1024).rearrange("p (c n) -> p c n", c=8) for _ in range(2)]
    ATs = [b.ab(512).rearrange("p (h n) -> p h n", h=4), b.ab1(512).rearrange("p (h n) -> p h n", h=4)]
    vbfs = [b.ab(1024), b.ab(1024)]
    gates = [b.af(1024), b.af1(1024)]
    gws = [b.af(1024), b.af(1024)]
    junks = [b.ab(1024), b.ab1(1024)]
    ogs = [b.ab(1024), b.ab1(1024)]
    ogTs = [b.ab(1024).rearrange("p (c n) -> p c n", c=8), b.ab1(1024).rearrange("p (c n) -> p c n", c=8)]
    gkTs = [gkT, b.af1(128)]
    b.memset("dve", gkTs[1][0:32, :], 1.0, w=[("gkT", 1)])
    sms = [b.af(16), b.af1(16)]
    mask01 = b.mask01
    for n in range(NB):
        r0 = n * 128
        par = n % 2

        def U(name, _p=par):
            return (name, _p)
        xt, uxt = xts[par], "xt%d" % par
        kd, ukd = kds[par], "kd%d" % par
        qkT, uqk = qkTs[par], "qkT%d" % par
        vbf, uvb = vbfs[par], "vbf%d" % par
        gw, ugw = gws[par], "gw%d" % par
        dec, udec = decs[:, par * 4:par * 4 + 4], "dec%d" % par
        hb, hT, et, lse, eb, enb, ebr = hbs[par], hTs[par], ets[par], lses[par], ebs[par], enbs[par], ebrs[par]
        qd, ki, AT, gate, junk, og, ogT, gkT, sm = qds[par], kis[par], ATs[par], gates[par], junks[par], ogs[par], ogTs[par], gkTs[par], sms[par]
        junk4 = junk
        tmp = {"ss": sm[:, 0:1], "lnv": sm[:, 1:2], "rstd": sm[:, 2:3], "junk": junk, "ujunk": U("junk"), "sfx": "_%d" % par}
        ss4 = sm[:, 4:8]
        ln4 = sm[:, 8:12]
        rs4 = sm[:, 12:16]
        b.dma("sp", xt, xin[r0:r0 + 128, :], uxt, w=[uxt])
        norm_block(b, xt, uxt, b.gv, "gv", hb, U("hb"), tmp)
        transpose_to(b, hb, U("hb"), hT, U("hT"), "act")
        pb, up = b.bank()
        for c in range(8):
            b.mm(pb[0:16, 0:128], win[:, c, 3072:3088], hT[:, c, :], c == 0, c == 7, r=["R2", "R2b", U("hT")], w=[up])
        b.cp("act", gkT[0:16, :], pb[0:16, 0:128], r=[up], w=[U("gkT")])
        pg, upg = b.bank()
        b.mm(pg, gkT[0:17, :], wgk[0:17, :], True, True, r=[U("gkT"), "wgk"], w=[upg])
        b.act(et, pg, AF.Exp, r=[upg], w=[U("et")], scale=-1.0)
        b.act(lse, et, AF.Ln, r=[U("et")], w=[U("lse")], bias=b.c_one, scale=1.0)
        pbb, upb = b.bank()
        b.mm(pbb, b.tu16, lse, True, True, r=[U("lse")], w=[upb])
        pbl, upl = b.bank()
        for h in range(4):
            b.mm(pbl[:, 2 * h:2 * h + 2], lse[:, h * 128:(h + 1) * 128], b.m16col2, True, True, r=[U("lse")], w=[upl])
        b.act(eb, pbb, AF.Exp, r=[upb], w=[U("eb")])
        b.act(enb, pbb, AF.Exp, r=[upb], w=[U("enb")], scale=-1.0)
        b.act(dec, pbl.rearrange("p (h two) -> p h two", two=2)[:, 0:4, 0], AF.Exp, r=[upl], w=[udec])
        pq, upq = b.bank()
        for c in range(8):
            b.mm(pq, hT[:, c, :], win[:, c, 0:512], c == 0, c == 7, r=["R2", "R2b", U("hT")], w=[upq])
        pk, upk = b.bank()
        for c in range(8):
            b.mm(pk, hT[:, c, :], win[:, c, 512:1024], c == 0, c == 7, r=["R2", "R2b", U("hT")], w=[upk])
        b.stt(qd, pq, float(128 ** -0.5), eb, ALU.mult, ALU.mult, r=[upq, U("eb")], w=[U("qd")])
        b.tt("dve", ki, pk, enb, ALU.mult, r=[upk, U("enb")], w=[U("ki")])
        for j in range(2):
            pv, upv = b.bank()
            for c in range(8):
                b.mm(pv, hT[:, c, :], win[:, c, 1024 + j * 512:1536 + j * 512], c == 0, c == 7, r=["R2", "R2b", U("hT")], w=[upv])
            b.cp("act", vbf[:, j * 512:(j + 1) * 512], pv, r=[upv], w=[(uvb, j)])
        for j in range(2):
            pr, upr2 = b.bank()
            for c in range(8):
                b.mm(pr, hT[:, c, :], win[:, c, 2048 + j * 512:2560 + j * 512], c == 0, c == 7, r=["R2", "R2b", U("hT")], w=[upr2])
            b.act(gate[:, j * 512:(j + 1) * 512], pr, AF.Silu, r=[upr2], w=[U(("gate", j))])
            b.tt("pool", gw[:, j * 512:(j + 1) * 512], gate[:, j * 512:(j + 1) * 512], gv2[:, j * 512:(j + 1) * 512],
                 ALU.mult, r=[U(("gate", j)), "gv2"], w=[(ugw, j)])
        pt, upt = b.bank()
        ptv = pt.bitcast(BF16).rearrange("p (c n) -> p c n", c=8)
        for h in range(4):
            b.tp(ptv[:, h, :], qd[:, h * 128:(h + 1) * 128], r=[U("qd")], w=[upt])
        for h in range(4):
            b.tp(ptv[:, 4 + h, :], ki[:, h * 128:(h + 1) * 128], r=[U("ki")], w=[upt])
        b.cp("dve", qkT, ptv, r=[upt], w=[uqk])
        pa, upa = b.bank()
        for h in range(4):
            b.mm(pa[:, h * 128:(h + 1) * 128], qkT[:, 4 + h, :], qkT[:, h, :], True, True, r=[uqk], w=[upa])
        for h in range(4):
            b.tt("dve", AT[:, h, :], pa[:, h * 128:(h + 1) * 128], mask01, ALU.mult, r=[upa], w=[U(("AT", h))])
        po = [b.bank(), b.bank()]
        for h in range(4):
            pob, upo = po[h // 2]
            cs_ = slice((h % 2) * 256, (h % 2) * 256 + 256)
            b.mm(pob[:, cs_], qkT[:, h, :], Sbf[:, h, :], True, False, r=[uqk, "Sbf"], w=[upo])
            b.mm(pob[:, cs_], AT[:, h, :], vbf[:, h * 256:(h + 1) * 256], False, True, r=[U(("AT", h)), (uvb, h // 2)], w=[upo])
        pkv = [b.bank(), b.bank()]
        for h in range(4):
            pkb, upkv = pkv[h // 2]
            cs_ = slice((h % 2) * 256, (h % 2) * 256 + 256)
            b.mm(pkb[:, cs_], ki[:, h * 128:(h + 1) * 128], vbf[:, h * 256:(h + 1) * 256], True, True, r=[U("ki"), (uvb, h // 2)], w=[upkv])
        for h in range(4):
            b.ts("dve", Sst[:, h, :], Sst[:, h, :], dec[:, h:h + 1], None, ALU.mult, ALU.bypass, r=["S", udec], w=["S"])
        for h in range(4):
            pkb, upkv = pkv[h // 2]
            cs_ = slice((h % 2) * 256, (h % 2) * 256 + 256)
            b.stt(Sst[:, h, :], pkb[:, cs_], dec[:, h:h + 1], Sst[:, h, :], ALU.mult, ALU.add, r=["S", udec, upkv], w=["S"])
        b.cp("pool", Sbf, Sst, r=["S"], w=["Sbf"])
        for h in range(4):
            pob, upo = po[h // 2]
            cs_ = slice((h % 2) * 256, (h % 2) * 256 + 256)
            b.act(junk4[:, h * 256:(h + 1) * 256], pob[:, cs_], AF.Square, r=[upo], w=[U(("ss4", h)), U("junk")], accum=ss4[:, h:h + 1])
        b.act(ln4, ss4, AF.Ln, r=[U(("ss4", h)) for h in range(4)], w=[U("ln4")], bias=b.c_eps, scale=1.0 / 256.0)
        b.act(rs4, ln4, AF.Exp, r=[U("ln4")], w=[U("rs4")], scale=-0.5)
        for h in range(4):
            pob, upo = po[h // 2]
            cs_ = slice((h % 2) * 256, (h % 2) * 256 + 256)
            b.stt(og[:, h * 256:(h + 1) * 256], pob[:, cs_], rs4[:, h:h + 1], gw[:, h * 256:(h + 1) * 256], ALU.mult, ALU.mult,
                  r=[upo, U("rs4"), (ugw, h // 2)], w=[U("og")])
        transpose_to(b, og, U("og"), ogT, U("ogT"), "act")
        for n2 in range(2):
            py, upy = b.bank()
            for c in range(8):
                b.mm(py, ogT[:, c, :], wo[:, c, n2 * 512:(n2 + 1) * 512], c == 0, c == 7, r=["R2", "R2b", U("ogT")], w=[upy])
            xs = xt[:, n2 * 512:(n2 + 1) * 512]
            b.tt("dve", xs, xs, py, ALU.add, r=[upy, uxt], w=[uxt])
        b.dma("sp", xout[r0:r0 + 128, :], xt, uxt, r=[uxt])
    b.S.barrier()


def phase_fox_proj(b, xin, wind, gd, gqd, gkd, bfd, scr, NPB=0, pbd=None, QS=0):
    T = b.T
    NT = T // 512
    b.areset()
    NIN = 4112
    win = b.R2[:, 0:8 * NIN].rearrange("p (c n) -> p c n", c=8)
    b.load_w(win, wind, 8, NIN, "R2", "R2")
    load_gvec(b, gd, b.gv, "gv")
    gq8 = b.af(1024)
    gk1 = b.af(1024)
    b.dma("sp", gq8, gqd, "gq", w=["gq8"])
    b.dma("sp", gk1, gkd, "gk", w=["gk1"])
    bfb = b.af(16)
    b.dma("sp", bfb, bfd, "bfb", w=["bfb"])
    mx = b.af(4)
    b.S.add("dve", lambda e: e.tensor_reduce(mx[:, 0:1], gq8[:, 0:64], AX.X, ALU.max, apply_absolute_value=True), reads=["gq8"], writes=["mxq"])
    b.S.add("dve", lambda e: e.tensor_reduce(mx[:, 1:2], gk1[:, 0:64], AX.X, ALU.max, apply_absolute_value=True), reads=["gk1"], writes=["mxk"])
    b.ts("dve", b.Mcol, mx[:, 0:1], mx[:, 1:2], 8.0, ALU.mult, ALU.mult, r=["mxq", "mxk"], w=["Mcol"])
    b.ts("dve", gq8, gq8, 0.125, None, ALU.mult, ALU.bypass, r=["gq8"], w=["gq8"])
    xt = b.af(1024)
    hb = b.ab(1024)
    hT = b.ab(1024).rearrange("p (c n) -> p c n", c=8)
    sq = b.af(1024)
    qtmp = b.af(1024)
    qn = b.ab(1024)
    gsb = b.ab(1024)
    qst = b.ab(8 * 512).rearrange("p (c n) -> p c n", c=8)
    kst = b.ab(8 * 512).rearrange("p (c n) -> p c n", c=8)
    gst = b.ab(8 * 512).rearrange("p (c n) -> p c n", c=8)
    vst = b.ab(16 * 4 * 64).rearrange("p (h k d) -> p h k d", h=16, k=4)
    cst3 = b.ab(512)
    sm = b.af(8)
    junk = b.ab(1024)
    tmp = {"ss": sm[:, 0:1], "lnv": sm[:, 1:2], "rstd": sm[:, 2:3], "junk": junk, "ujunk": "junk"}
    ss16 = b.af(16)
    ln16 = b.af(16)
    rs16 = b.af(16)
    ft = b.af(16)
    fe = b.af(16)
    lsp = b.af(16)
    r1 = b.af(16)
    r2 = b.af(16)
    c3 = b.ab(48)
    zer = b.af(16)
    b.memset("dve", zer, 0.0, w=["zer"])
    cs = b.cs
    for ti in range(NT):
        for k in range(4):
            n = ti * 4 + k
            r0 = n * 128
            b.dma("sp", xt, xin[r0:r0 + 128, :], "xt0", w=["xt"])
            norm_block(b, xt, "xt", b.gv, "gv", hb, "hb", tmp)
            transpose_to(b, hb, "hb", hT, "hT", "act")
            own = ti >= QS
            for which, (c0, gvec, ugv, stg, ustg) in enumerate(((0, gq8, "gq8", qst, "qst"), (1024, gk1, "gk1", kst, "kst"))):
                if which == 0 and not own:
                    continue
                pbs = []
                for j in range(2):
                    pb, up = b.bank()
                    for c in range(8):
                        b.mm(pb, hT[:, c, :], win[:, c, c0 + j * 512:c0 + (j + 1) * 512], c == 0, c == 7, r=["R2", "R2b", "hT"], w=[up])
                    b.act(sq[:, j * 512:(j + 1) * 512], pb, AF.Square, r=[up], w=[("sq", j)])
                    pbs.append((pb, up))
                b.S.add("dve", lambda e: e.tensor_reduce(ss16, sq.rearrange("p (h d) -> p h d", d=64), AX.X, ALU.add),
                        reads=[("sq", 0), ("sq", 1)], writes=["ss16"])
                b.act(ln16, ss16, AF.Ln, r=["ss16"], w=["ln16"], bias=b.c_eps, scale=1.0 / 64.0)
                b.act(rs16, ln16, AF.Exp, r=["ln16"], w=["rs16"], scale=-0.5)
                for j in range(2):
                    pb, up = pbs[j]
                    rsb = rs16[:, j * 8:(j + 1) * 8].unsqueeze(2).to_broadcast([128, 8, 64])
                    b.tt("dve", qtmp[:, j * 512:(j + 1) * 512].rearrange("p (h d) -> p h d", d=64),
                         pb.rearrange("p (h d) -> p h d", d=64), rsb, ALU.mult, r=[up, "rs16"], w=[("qtmp", j)])
                    b.tt("pool", qn[:, j * 512:(j + 1) * 512], qtmp[:, j * 512:(j + 1) * 512], gvec[:, j * 512:(j + 1) * 512],
                         ALU.mult, r=[("qtmp", j), ugv], w=[("qn", j)])
                pt, upt = b.bank()
                ptv = pt.bitcast(BF16).rearrange("p (c n) -> p c n", c=8)
                for c in range(8):
                    b.tp(ptv[:, c, :], qn[:, c * 128:(c + 1) * 128], r=[("qn", c // 4)], w=[upt])
                b.cp("act" if which else "dve", stg[:, :, k * 128:(k + 1) * 128], ptv, r=[upt], w=[ustg])
            for j in range(2):
                pb, up = b.bank()
                for c in range(8):
                    b.mm(pb, hT[:, c, :], win[:, c, 2048 + j * 512:2048 + (j + 1) * 512], c == 0, c == 7, r=["R2", "R2b", "hT"], w=[up])
                b.cp("act", vst[:, j * 8:(j + 1) * 8, k, :], pb.rearrange("p (h d) -> p h d", d=64), r=[up], w=["vst"])
            for j in range(2 if own else 0):
                pb, up = b.bank()
                for c in range(8):
                    b.mm(pb, hT[:, c, :], win[:, c, 3088 + j * 512:3088 + (j + 1) * 512], c == 0, c == 7, r=["R2", "R2b", "hT"], w=[up])
                b.act(gsb[:, j * 512:(j + 1) * 512], pb, AF.Sigmoid, r=[up], w=[("gsb", j)])
            if own:
                pt, upt = b.bank()
                ptv = pt.bitcast(BF16).rearrange("p (c n) -> p c n", c=8)
                for c in range(8):
                    b.tp(ptv[:, c, :], gsb[:, c * 128:(c + 1) * 128], r=[("gsb", c // 4)], w=[upt])
                b.cp("dve", gst[:, :, k * 128:(k + 1) * 128], ptv, r=[upt], w=["gst"])
            pf, upf = b.bank()
            for c in range(8):
                b.mm(pf[:, 0:16], hT[:, c, :], win[:, c, 3072:3088], c == 0, c == 7, r=["R2", "R2b", "hT"], w=[upf])
            b.tt("dve", ft, pf[:, 0:16], bfb, ALU.add, r=[upf, "bfb"], w=["ft"])
            b.act(fe, ft, AF.Exp, r=["ft"], w=["fe"], scale=-1.0)
            b.act(lsp, fe, AF.Ln, r=["fe"], w=["lsp"], bias=b.c_one, scale=1.0)
            pc, upc = b.bank()
            cprev = zer if n == 0 else cs[:, n - 1, :]
            b.mm(pc[:, 0:16], b.tum1, lsp, True, False, r=["lsp"], w=[upc])
            b.mm(pc[:, 0:16], b.emat, cprev, False, True, r=[("cs", n - 1), "zer"], w=[upc])
            b.cp("dve", cs[:, n, :], pc[:, 0:16], r=[upc], w=[("cs", n)])
            b.cp("dve", c3[:, 0:16], cs[:, n, :], r=[("cs", n)], w=["c3a"])
            b.tt("dve", r1, cs[:, n, :], c3[:, 0:16], ALU.subtract, r=[("cs", n), "c3a"], w=["r1"])
            b.cp("dve", c3[:, 16:32], r1, r=["r1"], w=["c3b"])
            b.tt("dve", r2, r1, c3[:, 16:32], ALU.subtract, r=["r1", "c3b"], w=["r2"])
            b.cp("dve", c3[:, 32:48], r2, r=["r2"], w=["c3c"])
            pt, upt = b.bank()
            ptb = pt.bitcast(BF16)
            b.tp(ptb[0:48, 0:128], c3, r=["c3a", "c3b", "c3c"], w=[upt])
            b.cp("dve", cst3[0:48, k * 128:(k + 1) * 128], ptb[0:48, 0:128], r=[upt], w=["cst3"])
        t0 = ti * 512
        for c in range(8):
            pass
        b.dma("sp", scr["KT"][:, t0:t0 + 512].rearrange("(c p) t -> p c t", p=128), kst, "kst", r=["kst"])
        b.dma("sp", scr["V"][:, :, ti * 4:ti * 4 + 4, :].rearrange("h p k d -> p h k d"), vst, "vst", r=["vst"])
        b.dma("sp", scr["CQ"][:, t0:t0 + 512], cst3[0:48, :], "cst3", r=["cst3"])
        if ti >= QS:
            b.dma("sp", scr["QT"][:, t0:t0 + 512].rearrange("(c p) t -> p c t", p=128), qst, "qst", r=["qst"])
            b.dma("sp", scr["GT"][:, t0:t0 + 512].rearrange("(c p) t -> p c t", p=128), gst, "gst", r=["gst"])
    tq = b.af(4)
    b.tt("dve", tq[:, 0:1], b.Mcol, b.c_m70, ALU.mult, r=["Mcol"], w=["tq0"])
    b.stt(tq[:, 1:2], tq[:, 0:1], -1.0, b.c_m1rows, ALU.mult, ALU.add, r=["tq0"], w=["tq1"])
    if NPB:
        pbc = b.af(1)
        b.dma("sp", pbc, pbd, "pbc", w=["pbc"])
        b.tt("dve", tq[:, 2:3], pbc[:, 0:1], b.c_m71, ALU.mult, r=["pbc"], w=["tq2"])
        b.tt("dve", b.augq, tq[:, 1:2], tq[:, 2:3], ALU.add, r=["tq1", "tq2"], w=["augq"])
    else:
        b.cp("dve", b.augq, tq[:, 1:2], r=["tq1"], w=["augq"])
    b.S.barrier()


def phase_fox_attn(b, scr, wod=None, w1d_prefetch=None, QS=0):
    T = b.T
    NT = T // 512
    NB = T // 128
    b.areset()
    if wod is not None:
        wo = b.R2[:, 0:8 * 1024].rearrange("p (c n) -> p c n", c=8)
        b.load_w(wo, wod, 8, 1024, "R2", "R2")
    if w1d_prefetch is not None:
        b.load_w(b.R1.rearrange("p (c n) -> p c n", c=8), w1d_prefetch, 8, 4096, "R1", "R1")
    qt = b.ab(T)
    kt = b.ab(T)
    va = b.ab(NB * 128).rearrange("p (n d) -> p n d", d=128)
    pts = [b.ab(512) for _ in range(3)]
    rden = b.af(512)
    ots = [b.ab(512) for _ in range(2)]
    NPBk = QS * 4 * 128
    b.memset("dve", kt[64:72, :], 1.0, w=["kt1"])
    b.ts("dve", kt[64:72, NPBk:T], kt[64:72, NPBk:T], b.c_kown[64:72, 0:1], None, ALU.mult, ALU.bypass, r=["kt1"], w=["kt1"])
    b.memset("dve", qt[64:72, :], 1.0, w=["qt1"])
    b.ts("dve", qt[64:72, :], qt[64:72, :], b.augq[64:72, 0:1], None, ALU.mult, ALU.bypass, r=["qt1"], w=["qt1"])
    b.memset("pool", va[:, :, 64:128], 1.0, w=["va1"])
    cqv = scr["CQ"].rearrange("(j h) t -> h j t", h=16)

    def load_chunk(h, c):
        cs_ = slice(c * 512, (c + 1) * 512)
        if c >= QS:
            b.dma("sp", qt[0:64, cs_], scr["QT"][h * 64:(h + 1) * 64, cs_], "aq%d" % c, w=[("qt", c)])
            b.dma("sp", qt[64:67, cs_], cqv[h][:, cs_], "aq%d" % c, r=["qt1"], w=[("qt", c)])
        b.dma("sp", kt[0:64, cs_], scr["KT"][h * 64:(h + 1) * 64, cs_], "ak%d" % c, w=[("kt", c)])
        b.dma("sp", kt[67:70, cs_], cqv[h][:, cs_], "ak%d" % c, r=["kt1"], w=[("kt", c)])
        b.dma("sp", va[:, c * 4:(c + 1) * 4, 0:64], scr["V"][h][:, c * 4:(c + 1) * 4, :], "av%d" % c, w=[("va", c)])

    blocks = []
    for h in range(16):
        for qi in reversed(range(QS, NT)):
            nkb = 4 * (qi + 1)
            for kb in range(nkb):
                blocks.append((h, qi, kb, nkb))

    def geom(blk):
        h, qi, kb, nkb = blk
        j = kb - 4 * qi
        q0 = 128 * j if j > 0 else 0
        return h, qi, kb, nkb, j, q0

    def emit_st(idx):
        h, qi, kb, nkb, j, q0 = geom(blocks[idx])
        sti = 2 + (idx % 4)
        ps_, ups = b.banks[sti], ("ps", sti)
        b.mm(ps_[:, q0:512], kt[0:72, kb * 128:(kb + 1) * 128], qt[0:72, qi * 512 + q0:(qi + 1) * 512], True, j < 0,
             r=[("kt", kb // 4), "kt1", ("qt", qi), "qt1"], w=[ups])
        if j >= 0:
            b.mm(ps_[:, q0:q0 + 128], b.ident, b.mneg, False, True, r=[], w=[ups])

    def emit_rest(idx):
        h, qi, kb, nkb, j, q0 = geom(blocks[idx])
        sti = 2 + (idx % 4)
        ps_, ups = b.banks[sti], ("ps", sti)
        po, upo = b.banks[qi % 2], ("ps", qi % 2)
        pt = pts[idx % 3]
        upt = ("pt", idx % 3)
        b.act(pt[:, q0:512], ps_[:, q0:512], AF.Exp, r=[ups], w=[upt])
        b.mm(po[:, q0:512], va[:, kb, :], pt[:, q0:512], kb == 0, kb == nkb - 1, r=[("va", kb // 4), "va1", upt], w=[upo])
        if kb == nkb - 1:
            b.S.add("dve", lambda e: e.reciprocal(rden[0:64, :], po[64:128, :]), reads=[upo], writes=["rden"], cost=4.4)
            ot = ots[qi % 2]
            b.tt("dve", ot[0:64, :], po[0:64, :], rden[0:64, :], ALU.mult, r=[upo, "rden"], w=[("ot", qi % 2)])
            b.dma("sp", scr["OT"][h * 64:(h + 1) * 64, qi * 512:(qi + 1) * 512], ot[0:64, :], "ot%d" % (qi % 2), r=[("ot", qi % 2)])
            if h + 1 < 16:
                load_chunk(h + 1, qi)
                if qi == QS:
                    for c in reversed(range(QS)):
                        load_chunk(h + 1, c)

    for c in reversed(range(NT)):
        load_chunk(0, c)
    LA = 2
    st_ptr = 0
    nblk = len(blocks)
    for idx in range(nblk):
        while st_ptr < nblk and st_ptr <= idx + LA and (st_ptr <= idx or blocks[st_ptr][0] == blocks[idx][0]):
            emit_st(st_ptr)
            st_ptr += 1
        emit_rest(idx)
    b.S.barrier()


def phase_fox_out(b, xin, xout, scr, wod=None, QS=0):
    T = b.T
    NT = T // 512
    b.areset()
    wo = b.R2[:, 0:8 * 1024].rearrange("p (c n) -> p c n", c=8)
    if wod is not None:
        b.load_w(wo, wod, 8, 1024, "R2", "R2")
    xt = b.af(4 * 1024).rearrange("p (k n) -> p k n", k=4)
    ot = b.ab(8 * 512).rearrange("p (c n) -> p c n", c=8)
    gt = b.ab(8 * 512).rearrange("p (c n) -> p c n", c=8)
    og = b.ab(8 * 512).rearrange("p (c n) -> p c n", c=8)
    for ti in range(QS, NT):
        t0 = ti * 512
        for k in range(4):
            r0 = (ti * 4 + k) * 128
            b.dma("sp", xt[:, k, :], xin[r0:r0 + 128, :], "xt%d" % k, w=[("xt", k)])
        b.dma("sp", ot, scr["OT"][:, t0:t0 + 512].rearrange("(c p) t -> p c t", p=128), "ot", w=["ot"])
        b.dma("sp", gt, scr["GT"][:, t0:t0 + 512].rearrange("(c p) t -> p c t", p=128), "gt", w=["gt"])
        for c in range(8):
            b.tt("pool" if c % 2 else "dve", og[:, c, :], ot[:, c, :], gt[:, c, :], ALU.mult, r=["ot", "gt"], w=[("og", c)])
        ogu = [("og", c) for c in range(8)]
        for k in range(4):
            for n2 in range(2):
                py, upy = b.bank()
                for c in range(8):
                    b.mm(py, og[:, c, k * 128:(k + 1) * 128], wo[:, c, n2 * 512:(n2 + 1) * 512], c == 0, c == 7, r=["R2", "R2b"] + ogu, w=[upy])
                xs = xt[:, k, n2 * 512:(n2 + 1) * 512]
                b.tt("dve", xs, xs, py, ALU.add, r=[upy, ("xt", k)], w=[("xt", k)])
            r0 = (ti * 4 + k) * 128
            b.dma("sp", xout[r0:r0 + 128, :], xt[:, k, :], "xt%d" % k, r=[("xt", k)])
    b.S.barrier()


WNAMES = [
    ("l0_w_in", [1024, 3088]), ("l0_w_out", [1024, 1024]), ("l0_w_ff1", [1024, 4096]), ("l0_w_ff2", [4096, 1024]),
    ("l1_w_in", [1024, 4112]), ("l1_w_out", [1024, 1024]), ("l1_w_ff1", [1024, 4096]), ("l1_w_ff2", [4096, 1024]),
]
VNAMES = [
    ("l0_norm_mix", 1024), ("l0_norm_ffn", 1024), ("l1_norm_mix", 1024), ("l1_norm_ffn", 1024), ("final_norm", 1024),
    ("l0_g_onorm", 1024), ("l1_g_q", 1024), ("l1_g_k", 1024), ("l1_b_f", 16),
]


def build_program(T, phases=("gla", "mlp0", "fox", "mlp1"), QS=0):
    nc = bass.Bass("TRN2", target_bir_lowering=False)
    dr = {}
    dr["x"] = nc.dram_tensor("x", [T, D], F32, kind="ExternalInput").ap()
    for nme, shp in WNAMES:
        dr[nme] = nc.dram_tensor(nme, shp, F32, kind="ExternalInput").ap()
    for nme, n in VNAMES:
        dr[nme] = nc.dram_tensor(nme, [128, n], F32, kind="ExternalInput").ap()
    dr["wgk"] = nc.dram_tensor("wgk", [17, 512], F32, kind="ExternalInput").ap()
    dr["cst"] = nc.dram_tensor("cst", [128, NCST], F32, kind="ExternalInput").ap()
    TO = T - QS * 512
    out_tiles = nc.dram_tensor("out", [TO, D], F32, kind="ExternalOutput").ap()
    dr["pbias"] = nc.dram_tensor("pbias", [128, 1], F32, kind="ExternalInput").ap()
    xr = nc.dram_tensor("xr", [T, D], F32, kind="Internal").ap()
    scr = {
        "QT": nc.dram_tensor("sQT", [1024, T], BF16, kind="Internal").ap(),
        "KT": nc.dram_tensor("sKT", [1024, T], BF16, kind="Internal").ap(),
        "GT": nc.dram_tensor("sGT", [1024, T], BF16, kind="Internal").ap(),
        "OT": nc.dram_tensor("sOT", [1024, T], BF16, kind="Internal").ap(),
        "V": nc.dram_tensor("sV", [16, 128, T // 128, 64], BF16, kind="Internal").ap(),
        "CQ": nc.dram_tensor("sCQ", [48, T], BF16, kind="Internal").ap(),
    }
    with ExitStack() as st:
        b = Bld(nc, st, T)
        b.R1 = b.sb("R1", [128, 8 * 4096], BF16)
        b.R2 = b.sb("R2", [128, 8 * 4112], BF16)
        b.ar = b.sb("arena", [128, AW], F32)
        cstf = b.sb("cstf", [128, NCST], F32)
        cstb = b.sb("cstb", [128, 256], BF16)
        b.gv = b.sb("gv", [128, 1024], F32)
        NB = T // 128
        b.cs = b.sb("cs", [128, NB, 16], F32)
        b.Mcol = b.sb("Mcol", [128, 1], F32)
        b.banks = [st.enter_context(nc.psum_tensor("bank%d" % i, [128, 512], F32))[:] for i in range(8)]
        b.gv = b.gv[:]
        b.cs = b.cs[:]
        b.Mcol = b.Mcol[:]
        b.R1 = b.R1[:]
        b.R2 = b.R2[:]
        b.dma("sp", cstf[:], dr["cst"], "cstf", w=["cstf"])
        b.dma("pool", cstb[:, 0:128], dr["cst"][:, C_ID:C_ID + 128], "cstb", w=["cstb"])
        b.dma("pool", cstb[:, 128:256], dr["cst"][:, C_MNEG:C_MNEG + 128], "cstb", w=["cstb"])
        b.S.barrier()
        b.ident = cstb[:, 0:128]
        b.mask01 = cstf[:, C_M01:C_M01 + 128]
        b.mneg = cstb[:, 128:256]
        b.tu16 = cstf[:, C_TU16:C_TU16 + 128]
        b.tl16 = cstf[:, C_TL16:C_TL16 + 128]
        b.tum1 = cstf[:, C_TUM1:C_TUM1 + 128]
        b.emat = cstf[:, C_E:C_E + 128]
        b.c_eps = cstf[:, C_COL:C_COL + 1]
        b.c_one = cstf[:, C_COL + 1:C_COL + 2]
        b.m16col2 = cstf[:, C_COL + 2:C_COL + 4]
        b.c_m1rows = cstf[:, C_COL + 4:C_COL + 5]
        b.c_m70 = cstf[:, C_COL + 5:C_COL + 6]
        b.c_m71 = cstf[:, C_COL + 6:C_COL + 7]
        b.c_kown = cstf[:, C_COL + 7:C_COL + 8]
        b.augq = b.sb("augq", [128, 1], F32)[:]
        cur = dr["x"]
        last = phases[-1]

        def dst(ph):
            return out_tiles if ph == last else xr

        for ph in phases:
            if ph == "gla":
                phase_gla(b, cur, dst(ph), dr["l0_w_in"], dr["wgk"], dr["l0_w_out"], dr["l0_norm_mix"], dr["l0_g_onorm"],
                          w1d_prefetch=None)
                cur = xr
            elif ph == "mlp0":
                phase_mlp(b, cur, dst(ph), dr["l0_w_ff1"], dr["l0_w_ff2"], dr["l0_norm_ffn"], w1_loaded=False)
                cur = xr
            elif ph == "fox":
                phase_fox_proj(b, cur, dr["l1_w_in"], dr["l1_norm_mix"], dr["l1_g_q"], dr["l1_g_k"], dr["l1_b_f"], scr,
                               NPB=QS * 4, pbd=dr["pbias"], QS=QS)
                phase_fox_attn(b, scr, wod=dr["l1_w_out"], w1d_prefetch=dr["l1_w_ff1"] if "mlp1" in phases else None, QS=QS)
                if ph == last:
                    assert QS == 0
                phase_fox_out(b, cur, dst(ph), scr, QS=QS)
                cur = xr
            elif ph == "mlp1":
                phase_mlp(b, cur, dst(ph), dr["l1_w_ff1"], dr["l1_w_ff2"], dr["l1_norm_ffn"], final_gd=dr["final_norm"],
                          w1_loaded=("fox" in phases), QS=QS, out_off=QS * 512)
                cur = xr
        b.S.barrier()
        b.S.run(st, reorder=(os.environ.get("NOREORDER") is None))
    return nc


def host_inputs(inputs):
    f = lambda a: np.ascontiguousarray(np.asarray(a, dtype=np.float32))
    m = {}
    for nme, _ in WNAMES:
        m[nme] = f(inputs[nme])
    bc = lambda v, reps=1: f(np.broadcast_to(np.tile(np.asarray(v, np.float32), reps)[None, :], (128, np.asarray(v).size * reps)))
    for nme in ("l0_norm_mix", "l0_norm_ffn", "l1_norm_mix", "l1_norm_ffn", "final_norm"):
        m[nme] = bc(inputs[nme])
    m["l0_g_onorm"] = bc(inputs["l0_g_onorm"], 4)
    m["l1_g_q"] = bc(inputs["l1_g_q"], 16)
    m["l1_g_k"] = bc(inputs["l1_g_k"], 16)
    m["l1_b_f"] = bc(inputs["l1_b_f"])
    m["wgk"] = f(np.concatenate([np.asarray(inputs["l0_w_gk_up"], np.float32), np.asarray(inputs["l0_b_gk"], np.float32)[None, :]], axis=0))
    m["cst"] = make_consts()
    return m


_PROG_CACHE = {}


def kernel(**inputs):
    x = np.asarray(inputs["x"], dtype=np.float32)
    B, S, _ = x.shape
    T = S
    H = S // 2
    QS = (S // 512) // 2
    key = (T, QS)
    if key not in _PROG_CACHE:
        _PROG_CACHE[key] = build_program(T, QS=QS)
    nc = _PROG_CACHE[key]
    shared = host_inputs(inputs)
    in_maps = []
    for bi in range(B):
        for r in range(2):
            m = dict(shared)
            if r == 0:
                m["x"] = np.ascontiguousarray(np.concatenate([np.zeros((H, D), np.float32), x[bi, :H]], axis=0))
                m["pbias"] = np.full((128, 1), -30000.0, np.float32)
            else:
                m["x"] = np.ascontiguousarray(x[bi])
                m["pbias"] = np.zeros((128, 1), np.float32)
            in_maps.append(m)
    res = run_bass_kernel_spmd(nc, in_maps, core_ids=list(range(2 * B)))
    out = np.empty((B, S, D), np.float32)
    for bi in range(B):
        for r in range(2):
            out[bi, r * H:(r + 1) * H] = np.asarray(res.results[2 * bi + r]["out"], dtype=np.float32)
    return out
```
